# Optimizing a Trainium2 kernel written in Bass

```python
import math
import jax, jax.numpy as jnp
from jax import lax
import numpy as np

D_MODEL = 1024
BATCH = 8
SEQ = 2048
DEPTH = 4

D_FF = ((8 * D_MODEL // 3 + 255) // 256) * 256
MIX_WIDTH = D_MODEL
SGU_WIDTH = MIX_WIDTH // 2
SGU_GROUP_DIM = 64
SGU_GROUPS = SGU_WIDTH // SGU_GROUP_DIM
CHUNK = 128
DIFF_WIDTH = MIX_WIDTH - SGU_WIDTH
DIFF_V_DIM = 128
DIFF_QK_DIM = DIFF_V_DIM // 2
DIFF_HEADS = DIFF_WIDTH // DIFF_V_DIM
IN_COLS = 2 * SGU_WIDTH + 3 * DIFF_WIDTH
Q_BLOCK = 128
EPS = 1e-6
NEG_INF = -1e30

kernel_name = "hybrid_sgu_diffattn_macaron"


def rmsnorm(x, gain):
    x32 = x.astype(jnp.float32)
    y = x32 * lax.rsqrt(jnp.mean(x32 * x32, axis=-1, keepdims=True) + EPS)
    return (y * gain.astype(jnp.float32)).astype(x.dtype)


def swiglu(h, w_gate, w_up, w_down):
    return (jax.nn.silu(h @ w_gate) * (h @ w_up)) @ w_down


def alibi_slopes(n_heads):
    i = jnp.arange(1, n_heads + 1, dtype=jnp.float32)
    return jnp.exp2(-8.0 * i / n_heads)


def spatial_gating(z, norm_gain, w_s, b_s):
    b, s, _ = z.shape
    nc = s // CHUNK
    u = z[..., :SGU_WIDTH].reshape(b, nc, CHUNK, SGU_GROUPS, SGU_GROUP_DIM)
    v = z[..., SGU_WIDTH:].reshape(b, s, SGU_GROUPS, SGU_GROUP_DIM)
    v = rmsnorm(v, norm_gain).reshape(b, nc, CHUNK, SGU_GROUPS, SGU_GROUP_DIM)
    causal = jnp.tril(jnp.ones((CHUNK, CHUNK), dtype=bool))
    w = jnp.where(causal[None], w_s, jnp.zeros_like(w_s))
    gate = jnp.einsum('gts,bcsgd->bctgd', w, v) + jnp.transpose(b_s)[None, None, :, :, None]
    return (u * gate).reshape(b, s, SGU_WIDTH)


def diff_attention(q, k, v, lam, slopes):
    b, s, h, _ = q.shape
    nb = s // Q_BLOCK
    scale = DIFF_QK_DIM ** -0.5
    qh = jnp.transpose(q, (0, 2, 1, 3)) * scale
    kh = jnp.transpose(k, (0, 2, 1, 3))
    vh = jnp.transpose(v, (0, 2, 1, 3))
    k1, k2 = kh[..., :DIFF_QK_DIM], kh[..., DIFF_QK_DIM:]
    qb = qh.reshape(b, h, nb, Q_BLOCK, 2 * DIFF_QK_DIM).transpose(2, 0, 1, 3, 4)
    kpos = jnp.arange(s)

    def block(args):
        qblk, i = args
        qpos = i * Q_BLOCK + jnp.arange(Q_BLOCK)
        dist = qpos[:, None] - kpos[None, :]
        bias = -slopes[:, None, None] * dist.astype(jnp.float32)[None]
        causal = dist >= 0

        def probs(qq, kk):
            sc = jnp.einsum('bhqd,bhkd->bhqk', qq, kk).astype(jnp.float32) + bias
            sc = jnp.where(causal, sc, NEG_INF)
            return jax.nn.softmax(sc, axis=-1)

        p = probs(qblk[..., :DIFF_QK_DIM], k1) - lam * probs(qblk[..., DIFF_QK_DIM:], k2)
        return jnp.einsum('bhqk,bhkd->bhqd', p.astype(vh.dtype), vh)

    o = lax.map(block, (qb, jnp.arange(nb)))
    return o.transpose(1, 0, 3, 2, 4).reshape(b, s, h, DIFF_V_DIM)


def setup_inputs(seed: int = 0) -> dict:
    key = jax.random.key(seed)
    ks = jax.random.split(key, 24)
    f32 = jnp.float32

    def nrm(k, shape, scale):
        return jax.random.normal(k, shape, f32) * scale

    def gain(k, shape):
        return 1.0 + 0.02 * jax.random.normal(k, shape, f32)

    return {
        "x": jax.random.normal(ks[0], (BATCH, SEQ, D_MODEL), f32),
        "ffn1_norm": gain(ks[1], (DEPTH, D_MODEL)),
        "ffn1_w_gate": nrm(ks[2], (DEPTH, D_MODEL, D_FF), D_MODEL ** -0.5),
        "ffn1_w_up": nrm(ks[3], (DEPTH, D_MODEL, D_FF), D_MODEL ** -0.5),
        "ffn1_w_down": nrm(ks[4], (DEPTH, D_FF, D_MODEL), D_FF ** -0.5),
        "mix_norm": gain(ks[5], (DEPTH, D_MODEL)),
        "w_in": nrm(ks[6], (DEPTH, D_MODEL, IN_COLS), D_MODEL ** -0.5),
        "sgu_norm": gain(ks[7], (DEPTH, SGU_GROUPS, SGU_GROUP_DIM)),
        "sgu_w": nrm(ks[8], (DEPTH, SGU_GROUPS, CHUNK, CHUNK), 0.5 * CHUNK ** -0.5),
        "sgu_b": 1.0 + 0.01 * jax.random.normal(ks[9], (DEPTH, SGU_GROUPS, CHUNK), f32),
        "lambda_q1": nrm(ks[10], (DEPTH, DIFF_QK_DIM), 0.1),
        "lambda_k1": nrm(ks[11], (DEPTH, DIFF_QK_DIM), 0.1),
        "lambda_q2": nrm(ks[12], (DEPTH, DIFF_QK_DIM), 0.1),
        "lambda_k2": nrm(ks[13], (DEPTH, DIFF_QK_DIM), 0.1),
        "diff_subln": gain(ks[14], (DEPTH, DIFF_V_DIM)),
        "w_out": nrm(ks[15], (DEPTH, MIX_WIDTH, D_MODEL), MIX_WIDTH ** -0.5),
        "ffn2_norm": gain(ks[16], (DEPTH, D_MODEL)),
        "ffn2_w_gate": nrm(ks[17], (DEPTH, D_MODEL, D_FF), D_MODEL ** -0.5),
        "ffn2_w_up": nrm(ks[18], (DEPTH, D_MODEL, D_FF), D_MODEL ** -0.5),
        "ffn2_w_down": nrm(ks[19], (DEPTH, D_FF, D_MODEL), D_FF ** -0.5),
        "final_norm": gain(ks[20], (D_MODEL,)),
    }


def reference(x, ffn1_norm, ffn1_w_gate, ffn1_w_up, ffn1_w_down, mix_norm, w_in,
              sgu_norm, sgu_w, sgu_b, lambda_q1, lambda_k1, lambda_q2, lambda_k2,
              diff_subln, w_out, ffn2_norm, ffn2_w_gate, ffn2_w_up, ffn2_w_down,
              final_norm):
    b, s, _ = x.shape
    slopes = alibi_slopes(DIFF_HEADS)
    for l in range(DEPTH):
        x = x + 0.5 * swiglu(rmsnorm(x, ffn1_norm[l]), ffn1_w_gate[l], ffn1_w_up[l], ffn1_w_down[l])

        h = rmsnorm(x, mix_norm[l])
        proj = h @ w_in[l]
        z_a = jax.nn.gelu(proj[..., :2 * SGU_WIDTH], approximate=False)
        y_a = spatial_gating(z_a, sgu_norm[l], sgu_w[l], sgu_b[l])

        off = 2 * SGU_WIDTH
        q = proj[..., off:off + DIFF_WIDTH].reshape(b, s, DIFF_HEADS, DIFF_V_DIM)
        k = proj[..., off + DIFF_WIDTH:off + 2 * DIFF_WIDTH].reshape(b, s, DIFF_HEADS, DIFF_V_DIM)
        v = proj[..., off + 2 * DIFF_WIDTH:off + 3 * DIFF_WIDTH].reshape(b, s, DIFF_HEADS, DIFF_V_DIM)
        lam_init = 0.8 - 0.6 * math.exp(-0.3 * l)
        lam = (jnp.exp(jnp.sum(lambda_q1[l].astype(jnp.float32) * lambda_k1[l].astype(jnp.float32)))
               - jnp.exp(jnp.sum(lambda_q2[l].astype(jnp.float32) * lambda_k2[l].astype(jnp.float32)))
               + lam_init)
        o = diff_attention(q, k, v, lam, slopes)
        o = rmsnorm(o, diff_subln[l]) * (1.0 - lam_init)
        y_b = o.reshape(b, s, DIFF_WIDTH)

        x = x + jnp.concatenate([y_a, y_b], axis=-1) @ w_out[l]

        x = x + 0.5 * swiglu(rmsnorm(x, ffn2_norm[l]), ffn2_w_gate[l], ffn2_w_up[l], ffn2_w_down[l])
    return rmsnorm(x, final_norm)
```

```python
import math
from contextlib import ExitStack
import numpy as np
import concourse.bass as bass
import concourse.mybir as mybir
from concourse.bass_utils import run_bass_kernel_spmd

F32 = mybir.dt.float32
BF16 = mybir.dt.bfloat16
I32 = mybir.dt.int32
AF = mybir.ActivationFunctionType
ALU = mybir.AluOpType
AX = mybir.AxisListType

D = 1024
S = 2048
L = 4
DFF = 2816
NG = 11
TT = 4
TW = 512
EPS = 1e-6
NSLOT = 4
SLOT_F32 = 3072
SLOPES = [2.0 ** (-8.0 * (i + 1) / 4) for i in range(4)]


class Res:
    __slots__ = ("name", "last_w", "readers", "excl")

    def __init__(self, name, excl=False):
        self.name = name
        self.last_w = None
        self.readers = []
        self.excl = excl

    def inherit(self, olds):
        for o in olds:
            if o.last_w is not None:
                self.readers.append(o.last_w)
            self.readers.extend(o.readers)
        return self


class Op:
    __slots__ = ("eng", "fn", "deps", "stream", "idx", "signal", "sigval", "dma_sem", "name")


class Prog:
    ENGS = ("pe", "act", "dve", "pool", "sp")

    def __init__(self, nc):
        self.nc = nc
        self.ops = {e: [] for e in self.ENGS}
        self.epoch = 0
        self.stream_idx = {}
        self.known = {e: {} for e in self.ENGS}
        self.n_waits = 0

    def set_epoch(self, e):
        self.epoch = e

    def op(self, eng, fn, reads=(), writes=(), dma_sem=None, name=""):
        o = Op()
        o.eng = eng
        o.fn = fn
        o.name = name
        o.signal = False
        o.sigval = None
        o.dma_sem = dma_sem
        if dma_sem is not None:
            o.stream = ("dma", dma_sem)
            o.signal = True
        else:
            o.stream = (eng, self.epoch)
        o.idx = self.stream_idx.get(o.stream, 0) + 1
        self.stream_idx[o.stream] = o.idx
        deps = []
        raw = set()
        for r in reads:
            if r.excl:
                continue
            if r.last_w is not None:
                deps.append(r.last_w)
                raw.add(id(r.last_w))
        wr = list(writes) + [r for r in reads if r.excl]
        for w in wr:
            if w.last_w is not None:
                deps.append(w.last_w)
                if w.excl:
                    raw.add(id(w.last_w))
            deps.extend(w.readers)
        best = {}
        for d in deps:
            if d.dma_sem is None and d.eng == eng and o.dma_sem is None:
                if eng == "pe" or id(d) not in raw:
                    continue
            if d.idx <= self.known[eng].get(d.stream, 0):
                continue
            b = best.get(d.stream)
            if b is None or d.idx > b.idx:
                best[d.stream] = d
        o.deps = list(best.values())
        for d in o.deps:
            d.signal = True
            self.known[eng][d.stream] = d.idx
        for r in reads:
            if not r.excl:
                r.readers.append(o)
        for w in wr:
            w.last_w = o
            w.readers = []
        self.ops[eng].append(o)
        return o

    def emit(self, final_streams=()):
        nc = self.nc
        totals = {}
        for e in self.ENGS:
            for o in self.ops[e]:
                if o.signal:
                    c = totals.get(o.stream, 0) + 1
                    totals[o.stream] = c
                    o.sigval = c
        names = sorted(totals.keys(), key=str)
        handles = {"pe": "tensor", "act": "scalar", "dve": "vector", "pool": "gpsimd", "sp": "sync"}
        with ExitStack() as st:
            sems = {}
            for s in names:
                sems[s] = st.enter_context(nc.semaphore("s_" + "_".join(str(x) for x in s).replace(":", "_")))
            block = st.enter_context(nc.Block())

            def wait_val(d):
                if d.dma_sem is not None:
                    if d.dma_sem.startswith("all:"):
                        return totals[d.stream] * 16
                    return d.sigval * 16
                return d.sigval

            def make(e):
                ops = self.ops[e]

                def body(eng):
                    for o in ops:
                        for d in o.deps:
                            eng.wait_ge(sems[d.stream], wait_val(d))
                            self.n_waits += 1
                        ins = o.fn(eng)
                        if o.signal:
                            ins.then_inc(sems[o.stream], 16 if o.dma_sem is not None else 1)
                    if e == "sp":
                        for stream in final_streams:
                            eng.wait_ge(sems[stream], totals[stream] * 16)
                return body

            for e in self.ENGS:
                if self.ops[e] or e == "sp":
                    getattr(block, handles[e])(make(e))
        return len(names)


class Region:
    def __init__(self, handle, nf32, name):
        self.h = handle
        self.n = nf32
        self.name = name
        self.off = 0
        self.cur = []
        self.prev = []

    def reset(self):
        self.prev = self.cur + self.prev[:0]
        self.cur = []
        self.off = 0

    def carve(self, name, free_shape, dtype, nres=1, at=None, res=None):
        n = 1
        for s in free_shape:
            n *= s
        nf = (n + 1) // 2 if dtype == BF16 else n
        nf = (nf + 7) // 8 * 8
        off = self.off if at is None else at
        assert off + nf <= self.n, (self.name, name, off, nf, self.n)
        a = self.h[:, off:off + nf]
        if dtype != F32:
            a = a.bitcast(dtype)
        a = a[:, 0:n]
        if at is None:
            self.off += nf
        self.last_off = off
        if len(free_shape) == 2:
            a = a.rearrange("p (a b) -> p a b", a=free_shape[0])
        elif len(free_shape) == 3:
            a = a.rearrange("p (a b c) -> p a b c", a=free_shape[0], b=free_shape[1])
        if res is None:
            res = [Res(f"{name}{i}").inherit(self.prev) for i in range(nres)]
            self.cur.extend(res)
        return a, res


def build(n_layers=L, stop="full", dbg=""):
    nc = bass.Bass("TRN2", target_bir_lowering=False)
    xT = nc.dram_tensor("xT", [D, S], F32, kind="ExternalInput").ap()
    wffn = nc.dram_tensor("wffn", [L * 2 * NG * 128, 6144], F32, kind="ExternalInput").ap()
    wproj = nc.dram_tensor("wproj", [L * 7 * 128, 4096], F32, kind="ExternalInput").ap()
    gains_d = nc.dram_tensor("gains", [128, L * 24 + 8], F32, kind="ExternalInput").ap()
    sgw_d = nc.dram_tensor("sgw", [L * 128, 1024], F32, kind="ExternalInput").ap()
    sgn_d = nc.dram_tensor("sgn", [L * 128, 512], F32, kind="ExternalInput").ap()
    sgb_d = nc.dram_tensor("sgb", [L * 128, 512], F32, kind="ExternalInput").ap()
    lamq_d = nc.dram_tensor("lamq", [128, L * 128], F32, kind="ExternalInput").ap()
    lamk_d = nc.dram_tensor("lamk", [128, L * 128], F32, kind="ExternalInput").ap()
    subln_d = nc.dram_tensor("subln", [128, L], F32, kind="ExternalInput").ap()
    outT = nc.dram_tensor("outT", [D, S], F32, kind="ExternalOutput").ap()

    P = Prog(nc)
    st = ExitStack()
    Xh = st.enter_context(nc.sbuf_tensor("X", [128, 8 * S], F32))
    RHh = st.enter_context(nc.sbuf_tensor("RH", [128, 8192], F32))
    RINGh = st.enter_context(nc.sbuf_tensor("RING", [128, NSLOT * SLOT_F32], F32))
    RMh = st.enter_context(nc.sbuf_tensor("RM", [128, 13312], F32))
    PARh = st.enter_context(nc.sbuf_tensor("PAR", [128, 1536], F32))
    CONh = st.enter_context(nc.sbuf_tensor("CON", [128, 1024], F32))
    PSh = st.enter_context(nc.psum_tensor("PS", [128, 4096], F32))

    X = Xh[:, :].rearrange("p (c t) -> p c t", c=8)
    Xres = [Res(f"X{t}") for t in range(TT)]
    bank = [PSh[:, b * 512:(b + 1) * 512] for b in range(8)]
    bres = [Res(f"bank{b}", excl=True) for b in range(8)]

    RH = Region(RHh, 8192, "RH")
    RM = Region(RMh, 13312, "RM")
    PAR = Region(PARh, 1536, "PAR")
    CON = Region(CONh, 1024, "CON")

    ONES, ONES_r = CON.carve("ones", [128], BF16)
    IDENT, IDENT_r = CON.carve("ident", [128], BF16)
    MASKB, MASKB_r = CON.carve("maskb", [128], BF16)
    TRI, TRI_r = CON.carve("tri", [128], F32)
    GAINS, GAINS_r = CON.carve("gains", [L * 24 + 8], F32)
    ALB, ALB_r = CON.carve("alb", [4, 16], F32)
    IOTA_I, IOTA_I_r = CON.carve("iota_i", [16], I32)
    IOTA_F, IOTA_F_r = CON.carve("iota_f", [16], F32)
    EPSC, EPSC_r = CON.carve("epsc", [1], F32)
    NEGLAM, NEGLAM_r = CON.carve("neglam", [L], F32)
    GSUB, GSUB_r = CON.carve("gsub", [L], F32)
    LSUM, LSUM_r = CON.carve("lsum", [2 * L], F32)
    LEXP, LEXP_r = CON.carve("lexp", [2 * L], F32)
    SUBLN, SUBLN_r = CON.carve("subln", [L], F32)
    OSQc, OSQc_r = CON.carve("osq", [TW], BF16)
    SSc, SSc_r = CON.carve("ss", [8], F32)
    SLc, SLc_r = CON.carve("sl", [8], F32)
    SRc, SRc_r = CON.carve("sr", [8], F32)
    SS2c, SS2c_r = CON.carve("ss2", [8], F32)
    SL2c, SL2c_r = CON.carve("sl2", [8], F32)
    SR2c, SR2c_r = CON.carve("sr2", [8], F32)
    SS2c_r, SL2c_r, SR2c_r = SS2c_r[0], SL2c_r[0], SR2c_r[0]
    SS4, SS4_r = CON.carve("ss4", [4, 8], F32)
    SL4, SL4_r = CON.carve("sl4", [4, 8], F32)
    SR4, SR4_r = CON.carve("sr4", [4, 8], F32)
    OSQc_r, SSc_r, SLc_r, SRc_r = OSQc_r[0], SSc_r[0], SLc_r[0], SRc_r[0]
    ONES_r, IDENT_r, MASKB_r, TRI_r, GAINS_r, ALB_r = ONES_r[0], IDENT_r[0], MASKB_r[0], TRI_r[0], GAINS_r[0], ALB_r[0]
    IOTA_I_r, IOTA_F_r, EPSC_r, NEGLAM_r, GSUB_r = IOTA_I_r[0], IOTA_F_r[0], EPSC_r[0], NEGLAM_r[0], GSUB_r[0]
    LSUM_r, LEXP_r, SUBLN_r = LSUM_r[0], LEXP_r[0], SUBLN_r[0]

    P.op("sp", lambda e: e.dma_start(out=GAINS, in_=gains_d), writes=[GAINS_r], dma_sem="all:par")
    P.op("sp", lambda e: e.dma_start(out=SUBLN, in_=subln_d), writes=[SUBLN_r], dma_sem="all:par")
    LQ, LQ_r = RM.carve("lq", [2 * L, 64], F32)
    LK, LK_r = RM.carve("lk", [2 * L, 64], F32)
    P.op("sp", lambda e: e.dma_start(out=LQ, in_=lamq_d.rearrange("p (a b) -> p a b", b=64)), writes=LQ_r, dma_sem="all:par")
    P.op("sp", lambda e: e.dma_start(out=LK, in_=lamk_d.rearrange("p (a b) -> p a b", b=64)), writes=LK_r, dma_sem="all:par")

    xsrc = xT.rearrange("(c p) t -> p c t", p=128)
    for t in range(TT):
        P.op("sp", lambda e, t=t: e.dma_start(out=X[:, :, t * TW:(t + 1) * TW], in_=xsrc[:, :, t * TW:(t + 1) * TW]),
             writes=[Xres[t]], dma_sem=f"x{t}")
    P.op("pool", lambda e: e.memset(ONES, 1.0), writes=[ONES_r])
    P.op("pool", lambda e: e.memset(IDENT, 1.0), writes=[IDENT_r])
    P.op("pool", lambda e: e.affine_select(out=IDENT, in_=IDENT, pattern=[[1, 128]], compare_op=ALU.is_equal,
                                           fill=0.0, base=0, channel_multiplier=-1),
         reads=[IDENT_r], writes=[IDENT_r])
    P.op("pool", lambda e: e.memset(MASKB, -30000.0), writes=[MASKB_r])
    P.op("pool", lambda e: e.affine_select(out=MASKB, in_=MASKB, pattern=[[-1, 128]], compare_op=ALU.is_gt,
                                           fill=0.0, base=0, channel_multiplier=1),
         reads=[MASKB_r], writes=[MASKB_r])
    P.op("pool", lambda e: e.memset(TRI, 1.0), writes=[TRI_r])
    P.op("pool", lambda e: e.affine_select(out=TRI, in_=TRI, pattern=[[1, 128]], compare_op=ALU.is_ge,
                                           fill=0.0, base=0, channel_multiplier=-1),
         reads=[TRI_r], writes=[TRI_r])
    P.op("pool", lambda e: e.memset(EPSC, EPS), writes=[EPSC_r])
    P.op("pool", lambda e: e.iota(IOTA_I, pattern=[[128, 16]], base=-14 * 128, channel_multiplier=1), writes=[IOTA_I_r])
    P.op("dve", lambda e: e.tensor_copy(out=IOTA_F, in_=IOTA_I), reads=[IOTA_I_r], writes=[IOTA_F_r])
    for h in range(4):
        P.op("dve", lambda e, h=h: e.tensor_scalar(out=ALB[:, h, :], in0=IOTA_F, scalar1=float(SLOPES[h]), scalar2=None,
                                                   op0=ALU.mult),
             reads=[IOTA_F_r], writes=[ALB_r])
    P.op("dve", lambda e: e.tensor_tensor(out=LQ, in0=LQ, in1=LK, op=ALU.mult), reads=LQ_r + LK_r, writes=LQ_r)
    P.op("dve", lambda e: e.tensor_reduce(out=LSUM, in_=LQ, axis=AX.X, op=ALU.add), reads=LQ_r, writes=[LSUM_r])
    P.op("act", lambda e: e.activation(out=LEXP, in_=LSUM, func=AF.Exp), reads=[LSUM_r], writes=[LEXP_r])
    for l in range(L):
        lam_init = 0.8 - 0.6 * math.exp(-0.3 * l)
        P.op("dve", lambda e, l=l, li=lam_init: e.tensor_scalar(out=NEGLAM[:, l:l + 1], in0=LEXP[:, 2 * l + 1:2 * l + 2],
                                                                scalar1=float(-li), scalar2=None, op0=ALU.add),
             reads=[LEXP_r], writes=[NEGLAM_r])
        P.op("dve", lambda e, l=l: e.tensor_tensor(out=NEGLAM[:, l:l + 1], in0=NEGLAM[:, l:l + 1], in1=LEXP[:, 2 * l:2 * l + 1],
                                                   op=ALU.subtract),
             reads=[LEXP_r, NEGLAM_r], writes=[NEGLAM_r])
        P.op("dve", lambda e, l=l, li=lam_init: e.tensor_scalar(out=GSUB[:, l:l + 1], in0=SUBLN[:, l:l + 1],
                                                                scalar1=float(1.0 - li), scalar2=None, op0=ALU.mult),
             reads=[SUBLN_r], writes=[GSUB_r])

    ring_res = [Res(f"slot{i}") for i in range(NSLOT)]
    loads = []
    for l in range(n_layers):
        for g in range(NG):
            loads.append(("ffn", ((l * 2 + 0) * NG + g) * 128, 6144))
        if stop == "ffn1" and l == n_layers - 1:
            break
        for pc in (2, 3):
            loads.append(("proj", (l * 7 + pc) * 128, 4096))
        for t in range(TT):
            order = [0, 1, 4] + ([2, 3] if t + 1 < TT else []) + [5, 6]
            for pc in order:
                loads.append(("proj", (l * 7 + pc) * 128, 4096))
        if stop == "mix" and l == n_layers - 1:
            break
        for g in range(NG):
            loads.append(("ffn", ((l * 2 + 1) * NG + g) * 128, 6144))
    ring_state = {"next_rec": 0, "next_use": 0, "released": 0}

    def slot_view(i, nelem):
        a = RINGh[:, i * SLOT_F32:(i + 1) * SLOT_F32].bitcast(BF16)
        return a[:, 0:nelem]

    def record_load(n):
        kind, row0, nelem = loads[n]
        si = n % NSLOT
        src = (wffn if kind == "ffn" else wproj)[row0:row0 + 128, :].rearrange("p (a b) -> p a b", b=2048)
        dst = slot_view(si, nelem).rearrange("p (a b) -> p a b", b=2048)
        P.op("pool", lambda e, dst=dst, src=src: e.dma_start(out=dst, in_=src, max_dma_last_dim=8192),
             writes=[ring_res[si]], dma_sem=f"w{si}")

    def ring_prefetch():
        while ring_state["next_rec"] < min(len(loads), ring_state["released"] + NSLOT):
            record_load(ring_state["next_rec"])
            ring_state["next_rec"] += 1

    def next_weights(kind):
        n = ring_state["next_use"]
        assert loads[n][0] == kind, (n, loads[n], kind)
        ring_prefetch()
        assert n < ring_state["next_rec"], (n, ring_state)
        ring_state["next_use"] = n + 1
        si = n % NSLOT
        return slot_view(si, loads[n][2]), ring_res[si]

    def release_weights():
        ring_state["released"] += 1
        assert ring_state["released"] <= ring_state["next_use"]
        ring_prefetch()

    def rms_stats(xtile_ap, xres, xsq, xsq_r, lbuf, lbuf_r, sb, nchunks, inv_n):
        P.op("act", lambda e: e.activation(out=xsq, in_=xtile_ap, func=AF.Square), reads=xres, writes=[xsq_r])
        for c in range(nchunks):
            rhs = xsq[:, c, :] if nchunks > 1 else xsq
            P.op("pe", lambda e, rhs=rhs, c=c: e.matmul(bank[sb], lhsT=ONES, rhs=rhs, start=(c == 0), stop=(c == nchunks - 1)),
                 reads=[ONES_r, xsq_r, bres[sb]])
        P.op("act", lambda e: e.activation(out=lbuf, in_=bank[sb], func=AF.Ln, bias=EPSC[:, 0:1], scale=float(inv_n)),
             reads=[bres[sb], EPSC_r], writes=[lbuf_r])
        P.op("act", lambda e: e.activation(out=bank[sb], in_=lbuf, func=AF.Exp, scale=-0.5),
             reads=[lbuf_r, bres[sb]])

    def norm_tile(t, gcol, xsq, xsq_r, lbuf, lbuf_r, sb, hout, hout_r):
        rms_stats(X[:, :, t * TW:(t + 1) * TW], [Xres[t]], xsq, xsq_r, lbuf, lbuf_r, sb, 8, 1.0 / D)
        for c in range(8):
            P.op("dve", lambda e, c=c: e.scalar_tensor_tensor(out=hout[:, c, :], in0=X[:, c, t * TW:(t + 1) * TW],
                                                             scalar=GAINS[:, gcol + c:gcol + c + 1], in1=bank[sb],
                                                             op0=ALU.mult, op1=ALU.mult),
                 reads=[Xres[t], GAINS_r, bres[sb]], writes=[hout_r])

    def ffn(l, f):
        RH.reset()
        RM.reset()
        H, H_r = RH.carve("H", [8, S], BF16, nres=TT)
        A = []
        for i in range(2):
            A.append(RM.carve(f"A{i}", [4, TW], BF16))
        SG = []
        for i in range(2):
            SG.append(RM.carve(f"SG{i}", [TW], BF16))
        XSQs = [RM.carve(f"XSQ{i}", [8, TW], BF16) for i in range(2)]
        LB, LB_r = RM.carve("LB", [TW], F32)
        gcol = l * 24 + (0 if f == 0 else 16)
        GB = [0, 1]
        UB = [2, 3]
        YB = [4, 5, 6, 7]
        SBK = YB[3]

        def n_square(t):
            xsq, xsq_r = XSQs[t % 2]
            P.op("act", lambda e: e.activation(out=xsq, in_=X[:, :, t * TW:(t + 1) * TW], func=AF.Square), reads=[Xres[t]], writes=xsq_r)

        def n_stat(t):
            xsq, xsq_r = XSQs[t % 2]
            for c in range(8):
                P.op("pe", lambda e, c=c: e.matmul(bank[SBK], lhsT=ONES, rhs=xsq[:, c, :], start=(c == 0), stop=(c == 7)),
                     reads=[ONES_r] + xsq_r + [bres[SBK]])
            P.op("act", lambda e: e.activation(out=LB, in_=bank[SBK], func=AF.Ln, bias=EPSC[:, 0:1], scale=1.0 / D),
                 reads=[bres[SBK], EPSC_r], writes=LB_r)
            P.op("act", lambda e: e.activation(out=bank[SBK], in_=LB, func=AF.Exp, scale=-0.5), reads=LB_r + [bres[SBK]])

        def n_apply(t):
            for c in range(8):
                P.op("dve", lambda e, c=c: e.scalar_tensor_tensor(out=H[:, c, t * TW:(t + 1) * TW], in0=X[:, c, t * TW:(t + 1) * TW],
                                                                 scalar=GAINS[:, gcol + c:gcol + c + 1], in1=bank[SBK],
                                                                 op0=ALU.mult, op1=ALU.mult),
                     reads=[Xres[t], GAINS_r, bres[SBK]], writes=[H_r[t]])

        def gu(sidx, t, Ws):
            a, a_r = A[sidx % 2]
            for gi, (W, W_r) in enumerate(Ws):
                WG = W[:, 0:2048].rearrange("p (c j) -> p c j", c=8)
                WU = W[:, 2048:4096].rearrange("p (c j) -> p c j", c=8)
                for j in range(2):
                    ci = 2 * gi + j
                    for (Wm, bk) in ((WG, GB[j]), (WU, UB[j])):
                        for c in range(8):
                            P.op("pe", lambda e, Wm=Wm, bk=bk, c=c, j=j: e.matmul(bank[bk], lhsT=Wm[:, c, j * 128:(j + 1) * 128],
                                                                                rhs=H[:, c, t * TW:(t + 1) * TW],
                                                                                start=(c == 0), stop=(c == 7)),
                                 reads=[W_r, H_r[t], bres[bk]])
                    sg, sg_r = SG[j]
                    P.op("act", lambda e, sg=sg, j=j: e.activation(out=sg, in_=bank[GB[j]], func=AF.Silu),
                         reads=[bres[GB[j]]], writes=sg_r)
                    P.op("dve", lambda e, sg=sg, j=j, a=a, ci=ci: e.tensor_tensor(out=a[:, ci, :], in0=bank[UB[j]], in1=sg, op=ALU.mult),
                         reads=[bres[UB[j]]] + sg_r, writes=a_r)

        ycount = [0]

        def yphase(sidx, t, Ws):
            a, a_r = A[sidx % 2]
            nch = 2 * len(Ws)
            for d in range(8):
                bk = YB[ycount[0] % 3]
                ycount[0] += 1
                k = 0
                for gi, (W, W_r) in enumerate(Ws):
                    WD = W[:, 4096:6144].rearrange("p (j d) -> p j d", j=2)
                    for j in range(2):
                        ci = 2 * gi + j
                        P.op("pe", lambda e, bk=bk, j=j, d=d, WD=WD, ci=ci, k=k: e.matmul(
                            bank[bk], lhsT=WD[:, j, d * 128:(d + 1) * 128], rhs=a[:, ci, :],
                            start=(k == 0), stop=(k == nch - 1)),
                            reads=[W_r] + a_r + [bres[bk]])
                        k += 1
                P.op("dve", lambda e, bk=bk, d=d: e.scalar_tensor_tensor(out=X[:, d, t * TW:(t + 1) * TW], in0=bank[bk], scalar=0.5,
                                                                         in1=X[:, d, t * TW:(t + 1) * TW],
                                                                         op0=ALU.mult, op1=ALU.add),
                     reads=[bres[bk], Xres[t]], writes=[Xres[t]])

        n_square(0)
        n_stat(0)
        n_apply(0)
        n_square(1)
        prev = None
        sidx = 0
        groups = list(range(NG))
        pairs = [groups[i:i + 2] for i in range(0, NG, 2)]
        for pi, pr in enumerate(pairs):
            Ws = [next_weights("ffn") for _ in pr]
            for t in range(TT):
                if pi == 0 and t + 1 < TT:
                    n_stat(t + 1)
                    n_apply(t + 1)
                    if t + 2 < TT:
                        n_square(t + 2)
                gu(sidx, t, Ws)
                if prev is not None:
                    yphase(*prev[:3])
                    if prev[1] == TT - 1:
                        for _ in prev[2]:
                            release_weights()
                prev = (sidx, t, Ws)
                sidx += 1
        yphase(*prev[:3])
        for _ in prev[2]:
            release_weights()

    def mixer(l):
        RH.reset()
        RM.reset()
        HT, HT_r = RH.carve("HT", [8, TW], BF16)
        K, K_r = RH.carve("K", [4, S], BF16, nres=TT)
        QZ, QZ_r = RH.carve("QZ", [2, 4, TW], BF16)
        V, V_r = RM.carve("V", [16, 512], BF16, nres=16)
        YM, YM_r = RM.carve("YM", [8, TW], BF16)
        ym_off = RM.last_off
        E = [[RM.carve(f"E{m}{i}", [TW], BF16) for i in range(2)] for m in range(2)]
        VN = [[RM.carve(f"VN{m}{i}", [512], BF16) for i in range(2)] for m in range(2)]
        XSQ, _ = RM.carve("XSQm", [8, TW], BF16, at=ym_off, res=YM_r)
        UG, U8_r = RM.carve("UG", [4, TW], F32)
        u8 = RM.last_off
        OOH, _ = RM.carve("OOH", [4, TW], F32, at=u8, res=U8_r)
        WST, _ = RM.carve("WST", [8, 128], F32, at=u8, res=U8_r)
        LB, LB_r = RM.carve("LBm", [TW], F32)
        VGs = [RM.carve(f"VG{i}", [8, 64], F32) for i in range(2)]
        RR, _ = RM.carve("RR", [2 * TW], F32, at=RM.last_off - 512, res=VGs[0][1])
        RR_r = VGs[0][1] + VGs[1][1]
        TMPs = [RM.carve(f"TMP{i}", [8, 64], F32) for i in range(2)]
        T2, _ = RM.carve("T2", [TW], F32, at=RM.last_off, res=TMPs[1][1])
        T2_r = TMPs[1][1]
        GG, GG_r = RM.carve("GG", [TW], F32)
        OSQ, OSQ_r = OSQc, [OSQc_r]
        PAR.reset()
        WS, WS_r = PAR.carve("WS", [8, 128], BF16)
        SGN, SGN_r = PAR.carve("SGN", [8, 64], F32)
        SGB, SGB_r = PAR.carve("SGB", [4, 128], F32)
        gcol = l * 24 + 8

        P.op("sp", lambda e: e.dma_start(out=SGN, in_=sgn_d[l * 128:(l + 1) * 128, :].rearrange("p (a b) -> p a b", a=8)),
             writes=SGN_r, dma_sem=f"all:lp{l}")
        P.op("sp", lambda e: e.dma_start(out=SGB, in_=sgb_d[l * 128:(l + 1) * 128, :].rearrange("p (a b) -> p a b", a=4)),
             writes=SGB_r, dma_sem=f"all:lp{l}")
        P.op("sp", lambda e: e.dma_start(out=WST, in_=sgw_d[l * 128:(l + 1) * 128, :].rearrange("p (a b) -> p a b", a=8)),
             writes=U8_r, dma_sem=f"all:lp{l}")
        P.op("dve", lambda e: e.tensor_tensor(out=WS, in0=WST, in1=TRI.unsqueeze(1).broadcast_to([128, 8, 128]), op=ALU.mult),
             reads=U8_r + [TRI_r], writes=WS_r)
        P.op("pool", lambda e: e.memset(QZ, 0.0), writes=QZ_r)
        for m in range(2):
            for i in range(2):
                P.op("pool", lambda e, m=m, i=i: e.memset(VN[m][i][0], 0.0), writes=VN[m][i][1])

        cp = [0]
        pending = [False]

        def rel():
            if pending[0]:
                release_weights()
                pending[0] = False

        def nxt():
            rel()
            pending[0] = True
            return next_weights("proj")

        def evac(out, src_bank, src_ap, out_res, reads_extra=()):
            if cp[0] % 2 == 0:
                P.op("dve", lambda e: e.tensor_copy(out=out, in_=src_ap), reads=[bres[src_bank]] + list(reads_extra), writes=out_res)
            else:
                P.op("act", lambda e: e.copy(out=out, in_=src_ap), reads=[bres[src_bank]] + list(reads_extra), writes=out_res)
            cp[0] += 1

        def norm_stats(t, sbk):
            xt = X[:, :, t * TW:(t + 1) * TW]
            P.op("act", lambda e: e.activation(out=XSQ, in_=xt, func=AF.Square), reads=[Xres[t]], writes=YM_r)
            for c in range(8):
                P.op("pe", lambda e, c=c: e.matmul(bank[sbk], lhsT=ONES, rhs=XSQ[:, c, :], start=(c == 0), stop=(c == 7)),
                     reads=[ONES_r] + YM_r + [bres[sbk]])
            P.op("act", lambda e: e.activation(out=bank[sbk], in_=bank[sbk], func=AF.Ln, bias=EPSC[:, 0:1], scale=1.0 / D),
                 reads=[bres[sbk], EPSC_r])
            P.op("act", lambda e: e.activation(out=LB, in_=bank[sbk], func=AF.Exp, scale=-0.5), reads=[bres[sbk]], writes=LB_r)

        def h_stts(t):
            for c in range(8):
                P.op("dve", lambda e, c=c: e.scalar_tensor_tensor(out=HT[:, c, :], in0=X[:, c, t * TW:(t + 1) * TW],
                                                                 scalar=GAINS[:, gcol + c:gcol + c + 1], in1=LB,
                                                                 op0=ALU.mult, op1=ALU.mult),
                     reads=[Xres[t], GAINS_r] + LB_r, writes=HT_r)

        def q_proj(t):
            W, W_r = nxt()
            WC = W.rearrange("p (c j) -> p c j", c=8)
            for h in range(4):
                bk = 4 + h
                for c in range(8):
                    P.op("pe", lambda e, bk=bk, c=c, h=h, WC=WC: e.matmul(bank[bk], lhsT=WC[:, c, h * 128:(h + 1) * 128], rhs=HT[:, c, :],
                                                                      start=(c == 0), stop=(c == 7)),
                         reads=[W_r, HT_r[0], bres[bk]])
            for h in range(4):
                bk = 4 + h
                cp[0] = h
                evac(QZ[0:64, 0, h, :], bk, bank[bk][0:64, :], QZ_r)
                cp[0] = h
                evac(QZ[64:128, 1, h, :], bk, bank[bk][64:128, :], QZ_r)

        def k_proj(t):
            W, W_r = nxt()
            WD_ = W.rearrange("p (c j) -> p c j", c=8)
            for h in range(4):
                bk = 4 + h
                for c in range(8):
                    P.op("pe", lambda e, bk=bk, c=c, h=h, WD_=WD_: e.matmul(bank[bk], lhsT=WD_[:, c, h * 128:(h + 1) * 128], rhs=HT[:, c, :],
                                                                        start=(c == 0), stop=(c == 7)),
                         reads=[W_r, HT_r[0], bres[bk]])
            for h in range(4):
                bk = 4 + h
                evac(K[:, h, t * TW:(t + 1) * TW], bk, bank[bk], [K_r[t]])

        norm_stats(0, 0)
        h_stts(0)
        q_proj(0)
        k_proj(0)
        for t in range(TT):
            tok = slice(t * TW, (t + 1) * TW)
            W, W_r = nxt()
            WA = W.rearrange("p (c j) -> p c j", c=8)
            for j in range(4):
                bk = 4 + j % 2
                for c in range(8):
                    P.op("pe", lambda e, bk=bk, c=c, j=j, WA=WA: e.matmul(bank[bk], lhsT=WA[:, c, j * 128:(j + 1) * 128], rhs=HT[:, c, :],
                                                                      start=(c == 0), stop=(c == 7)),
                         reads=[W_r, HT_r[0], bres[bk]])
                P.op("act", lambda e, bk=bk, j=j: e.activation(out=UG[:, j, :], in_=bank[bk], func=AF.Gelu),
                     reads=[bres[bk]], writes=U8_r)
            W, W_r = nxt()
            WB = W.rearrange("p (c j) -> p c j", c=8)
            for i in range(4):
                for c in range(8):
                    P.op("pe", lambda e, c=c, i=i, WB=WB: e.matmul(bank[i], lhsT=HT[:, c, i * 128:(i + 1) * 128], rhs=WB[:, c, :],
                                                              start=(c == 0), stop=(c == 7)),
                         reads=[W_r, HT_r[0], bres[i]])
            for i in range(4):
                tm, tm_r = TMPs[i % 2]
                b3 = bank[i].rearrange("p (a b) -> p a b", a=8)
                P.op("act", lambda e, b3=b3: e.activation(out=b3, in_=b3, func=AF.Gelu), reads=[bres[i]])
                P.op("act", lambda e, b3=b3, tm=tm: e.activation(out=tm, in_=b3, func=AF.Square), reads=[bres[i]], writes=tm_r)
                P.op("dve", lambda e, tm=tm, i=i: e.tensor_reduce(out=SS4[:, i, :], in_=tm, axis=AX.X, op=ALU.add), reads=tm_r, writes=SS4_r)
            P.op("act", lambda e: e.activation(out=SL4, in_=SS4, func=AF.Ln, bias=EPSC[:, 0:1], scale=1.0 / 64), reads=SS4_r + [EPSC_r], writes=SL4_r)
            P.op("act", lambda e: e.activation(out=SR4, in_=SL4, func=AF.Exp, scale=-0.5), reads=SL4_r, writes=SR4_r)

            def vn_chain(i):
                tm, tm_r = TMPs[i % 2]
                b3 = bank[i].rearrange("p (a b) -> p a b", a=8)
                P.op("dve", lambda e: e.tensor_tensor(out=tm, in0=b3, in1=SR4[:, i, :].unsqueeze(2).broadcast_to([128, 8, 64]), op=ALU.mult),
                     reads=[bres[i]] + SR4_r, writes=tm_r)
                vna, vna_r = VN[0][i % 2]
                vnb, vnb_r = VN[1][i % 2]
                T4 = tm.rearrange("p (a two) d -> p a two d", two=2)
                G4 = SGN.rearrange("p (a two) d -> p a two d", two=2)
                A4 = vna.rearrange("p (a two d) -> p a two d", two=2, d=64)
                B4 = vnb.rearrange("p (a two d) -> p a two d", two=2, d=64)
                P.op("dve", lambda e: e.tensor_tensor(out=A4[:, :, 0, :], in0=T4[:, :, 0, :], in1=G4[:, :, 0, :], op=ALU.mult),
                     reads=tm_r + SGN_r, writes=vna_r)
                P.op("dve", lambda e: e.tensor_tensor(out=B4[:, :, 1, :], in0=T4[:, :, 1, :], in1=G4[:, :, 1, :], op=ALU.mult),
                     reads=tm_r + SGN_r, writes=vnb_r)

            def gates(i):
                vna, vna_r = VN[0][i % 2]
                vnb, vnb_r = VN[1][i % 2]
                for j in range(4):
                    gb = 4 + j
                    P.op("pe", lambda e, gb=gb, j=j: e.matmul(bank[gb][:, i * 128:(i + 1) * 128], lhsT=vna[:, j * 128:(j + 1) * 128],
                                                            rhs=WS[:, 2 * j, :], start=True, stop=False),
                         reads=vna_r + WS_r + [bres[gb]])
                    P.op("pe", lambda e, gb=gb, j=j: e.matmul(bank[gb][:, i * 128:(i + 1) * 128], lhsT=vnb[:, j * 128:(j + 1) * 128],
                                                            rhs=WS[:, 2 * j + 1, :], start=False, stop=True),
                         reads=vnb_r + WS_r + [bres[gb]])

            if t + 1 < TT:
                xt1 = X[:, :, (t + 1) * TW:(t + 2) * TW]
                P.op("act", lambda e, xt1=xt1: e.activation(out=XSQ, in_=xt1, func=AF.Square), reads=[Xres[t + 1]], writes=YM_r)
            W, W_r = nxt()
            WE = W.rearrange("p (c j) -> p c j", c=8)
            for i in range(4):
                bk = 4 + i
                for c in range(8):
                    P.op("pe", lambda e, bk=bk, c=c, i=i, WE=WE: e.matmul(bank[bk], lhsT=HT[:, c, i * 128:(i + 1) * 128], rhs=WE[:, c, :],
                                                                      start=(c == 0), stop=(c == 7)),
                         reads=[W_r, HT_r[0], bres[bk]])
            vn_chain(0)
            vn_chain(1)
            for i in range(4):
                evac(V[:, 4 * t + i, :], 4 + i, bank[4 + i], [V_r[4 * t + i]])
            gates(0)
            gates(1)
            vn_chain(2)
            vn_chain(3)
            if t + 1 < TT:
                for c in range(8):
                    P.op("pe", lambda e, c=c: e.matmul(bank[0], lhsT=ONES, rhs=XSQ[:, c, :], start=(c == 0), stop=(c == 7)),
                         reads=[ONES_r] + YM_r + [bres[0]])
                P.op("act", lambda e: e.activation(out=bank[0], in_=bank[0], func=AF.Ln, bias=EPSC[:, 0:1], scale=1.0 / D),
                     reads=[bres[0], EPSC_r])
                P.op("act", lambda e: e.activation(out=LB, in_=bank[0], func=AF.Exp, scale=-0.5), reads=[bres[0]], writes=LB_r)
            gates(2)
            gates(3)
            for j in range(4):
                gb = 4 + j
                P.op("dve", lambda e, gb=gb, j=j: e.tensor_tensor(out=GG.rearrange("p (a b) -> p a b", a=4),
                                                                 in0=bank[gb].rearrange("p (a b) -> p a b", a=4),
                                                                 in1=SGB[:, j, :].unsqueeze(1).broadcast_to([128, 4, 128]), op=ALU.add),
                     reads=[bres[gb]] + SGB_r, writes=GG_r)
                P.op("dve", lambda e, j=j: e.tensor_tensor(out=YM[:, j, :], in0=GG, in1=UG[:, j, :], op=ALU.mult),
                     reads=GG_r + U8_r, writes=YM_r)
            if t + 1 < TT:
                h_stts(t + 1)
            rel()
            nkt = 4 * t + 4

            def qk(p, h, kt):
                jd = kt - 4 * t
                c0 = 128 * jd if jd >= 0 else 0
                cols = slice(c0, TW)
                sb = [p % 2, 2 + p % 2]
                bcol = kt - 4 * t + 12
                for m in range(2):
                    P.op("pe", lambda e, m=m, sbm=sb[m]: e.matmul(
                        bank[sbm][:, cols], lhsT=K[:, h, kt * 128:(kt + 1) * 128], rhs=QZ[:, m, h, cols],
                        start=True, stop=(jd < 0)),
                        reads=[K_r[kt // 4]] + QZ_r + [bres[sb[m]]])
                    if jd >= 0:
                        P.op("pe", lambda e, sbm=sb[m]: e.matmul(bank[sbm][:, c0:c0 + 128], lhsT=IDENT, rhs=MASKB,
                                                                  start=False, stop=True),
                             reads=[IDENT_r, MASKB_r, bres[sb[m]]])
                    em, em_r = E[m][p % 2]
                    P.op("act", lambda e, em=em, sbm=sb[m]: e.activation(
                        out=em[:, cols], in_=bank[sbm][:, cols], func=AF.Exp, bias=ALB[:, h, bcol:bcol + 1], scale=0.125),
                        reads=[bres[sb[m]], ALB_r], writes=em_r)

            def pv(p, h, kt, nkt=nkt):
                jd = kt - 4 * t
                c0 = 128 * jd if jd >= 0 else 0
                cols = slice(c0, TW)
                for m in range(2):
                    em, em_r = E[m][p % 2]
                    P.op("pe", lambda e, m=m, em=em: e.matmul(
                        bank[4 + m][:, cols], lhsT=V[:, kt, h * 128:(h + 1) * 128], rhs=em[:, cols],
                        start=(kt == 0), stop=(kt == nkt - 1)),
                        reads=[V_r[kt]] + em_r + [bres[4 + m]])
                    P.op("pe", lambda e, m=m, em=em: e.matmul(
                        bank[6 + m][:, cols], lhsT=ONES, rhs=em[:, cols], start=(kt == 0), stop=(kt == nkt - 1)),
                        reads=[ONES_r] + em_r + [bres[6 + m]])
                if kt == nkt - 1:
                    finalize(h)

            def finalize(h):
                oo = OOH[:, h, :]
                big = SLOPES[h] * 256 > 30
                if big:
                    P.op("act", lambda e: e.copy(out=RR, in_=PSh[:, 6 * 512:8 * 512]), reads=[bres[6], bres[7]], writes=RR_r)
                else:
                    P.op("act", lambda e: e.activation(out=RR, in_=PSh[:, 6 * 512:8 * 512], func=AF.Ln), reads=[bres[6], bres[7]], writes=RR_r)
                P.op("dve", lambda e: e.tensor_copy(out=oo, in_=bank[4]), reads=[bres[4]], writes=U8_r)
                P.op("dve", lambda e: e.tensor_copy(out=T2, in_=bank[5]), reads=[bres[5]], writes=T2_r)
                if big:
                    P.op("dve", lambda e: e.reciprocal(out=RR, in_=RR), reads=RR_r, writes=RR_r)
                else:
                    P.op("act", lambda e: e.activation(out=RR, in_=RR, func=AF.Exp, scale=-1.0), reads=RR_r, writes=RR_r)
                P.op("dve", lambda e: e.tensor_tensor(out=oo, in0=oo, in1=RR[:, 0:TW], op=ALU.mult), reads=U8_r + RR_r, writes=U8_r)
                P.op("dve", lambda e: e.tensor_tensor(out=T2, in0=T2, in1=RR[:, TW:2 * TW], op=ALU.mult), reads=T2_r + RR_r, writes=T2_r)
                P.op("dve", lambda e: e.scalar_tensor_tensor(out=oo, in0=T2, scalar=NEGLAM[:, l:l + 1], in1=oo, op0=ALU.mult, op1=ALU.add),
                     reads=T2_r + U8_r + [NEGLAM_r], writes=U8_r)

            ebufs = [E[0][0], E[0][1], E[1][0], E[1][1]]

            def tail_squares():
                for h in range(4):
                    eb, eb_r = ebufs[h]
                    P.op("act", lambda e, h=h, eb=eb: e.activation(out=eb, in_=OOH[:, h, :], func=AF.Square), reads=U8_r, writes=eb_r)

            def tail_stats():
                for h in range(4):
                    eb, eb_r = ebufs[h]
                    P.op("pe", lambda e, h=h, eb=eb: e.matmul(bank[h], lhsT=ONES, rhs=eb, start=True, stop=True), reads=[ONES_r] + eb_r + [bres[h]])
                P.op("act", lambda e: e.activation(out=PSh[:, 0:2048], in_=PSh[:, 0:2048], func=AF.Ln, bias=EPSC[:, 0:1], scale=1.0 / 128),
                     reads=[bres[0], bres[1], bres[2], bres[3], EPSC_r])
                P.op("act", lambda e: e.activation(out=PSh[:, 0:2048], in_=PSh[:, 0:2048], func=AF.Exp, scale=-0.5),
                     reads=[bres[0], bres[1], bres[2], bres[3]])

            def tail_apply():
                for h in range(4):
                    P.op("dve", lambda e, h=h: e.scalar_tensor_tensor(out=YM[:, 4 + h, :], in0=OOH[:, h, :], scalar=GSUB[:, l:l + 1], in1=bank[h],
                                                                      op0=ALU.mult, op1=ALU.mult),
                         reads=U8_r + [bres[h], GSUB_r], writes=YM_r)

            seq = [(h, kt) for h in range(4) for kt in range(nkt)]
            for p, (h, kt) in enumerate(seq):
                qk(p, h, kt)
                if p > 0:
                    pv(p - 1, *seq[p - 1])
            pv(len(seq) - 1, *seq[-1])
            rel()
            tail_squares()
            if t + 1 < TT:
                q_proj(t + 1)
            tail_stats()
            tail_apply()
            if t + 1 < TT:
                k_proj(t + 1)
            if dbg == "nosgu":
                P.op("dve", lambda e: e.memset(YM[:, 0:4, :], 0.0), reads=YM_r, writes=YM_r)
            if dbg == "noattn":
                P.op("dve", lambda e: e.memset(YM[:, 4:8, :], 0.0), reads=YM_r, writes=YM_r)
            for half in range(2):
                W, W_r = nxt()
                WO = W.rearrange("p (c j) -> p c j", c=8)
                for dd in range(4):
                    d = half * 4 + dd
                    bk = 4 + d % 4
                    for c in range(8):
                        P.op("pe", lambda e, bk=bk, c=c, dd=dd, WO=WO: e.matmul(bank[bk], lhsT=WO[:, c, dd * 128:(dd + 1) * 128], rhs=YM[:, c, :],
                                                                            start=(c == 0), stop=(c == 7)),
                             reads=[W_r] + YM_r + [bres[bk]])
                    P.op("dve", lambda e, bk=bk, d=d, tok=tok: e.tensor_tensor(out=X[:, d, tok], in0=bank[bk], in1=X[:, d, tok], op=ALU.add),
                         reads=[bres[bk], Xres[t]], writes=[Xres[t]])
            rel()

    done = False
    for l in range(n_layers):
        P.set_epoch(l)
        last = (l == n_layers - 1)
        ffn(l, 0)
        if last and stop == "ffn1":
            break
        mixer(l)
        if last and stop == "mix":
            break
        ffn(l, 1)

    RH.reset()
    RM.reset()
    odst = outT.rearrange("(c p) t -> p c t", p=128)
    if stop == "full":
        XSQ, XSQ_r = RM.carve("XSQf", [8, TW], BF16)
        LB, LB_r = RM.carve("LBf", [TW], F32)
        for t in range(TT):
            tok = slice(t * TW, (t + 1) * TW)
            sb = 4 + t
            rms_stats(X[:, :, tok], [Xres[t]], XSQ, XSQ_r[0], LB, LB_r[0], sb, 8, 1.0 / D)
            for c in range(8):
                P.op("dve", lambda e, c=c, tok=tok, sb=sb: e.scalar_tensor_tensor(out=X[:, c, tok], in0=X[:, c, tok],
                                                                                scalar=GAINS[:, L * 24 + c:L * 24 + c + 1], in1=bank[sb],
                                                                                op0=ALU.mult, op1=ALU.mult),
                     reads=[Xres[t], GAINS_r, bres[sb]], writes=[Xres[t]])
            P.op("sp", lambda e, tok=tok: e.dma_start(out=odst[:, :, tok], in_=X[:, :, tok]), reads=[Xres[t]], dma_sem="all:out")
    else:
        for t in range(TT):
            tok = slice(t * TW, (t + 1) * TW)
            P.op("sp", lambda e, tok=tok: e.dma_start(out=odst[:, :, tok], in_=X[:, :, tok]), reads=[Xres[t]], dma_sem="all:out")
    assert ring_state["next_use"] == len(loads), (ring_state, len(loads))
    nsem = P.emit(final_streams=[("dma", "all:out")])
    st.close()
    return nc, P, nsem


def prep_inputs(inp):
    f = lambda a: np.ascontiguousarray(np.asarray(a, dtype=np.float32))
    shared = {}
    wffn = np.empty((L, 2, NG, 128, 6144), np.float32)
    for fi, pre in enumerate(("ffn1", "ffn2")):
        wg = f(inp[pre + "_w_gate"]).reshape(L, 8, 128, NG, 256)
        wu = f(inp[pre + "_w_up"]).reshape(L, 8, 128, NG, 256)
        wd = f(inp[pre + "_w_down"]).reshape(L, NG, 2, 128, 1024)
        wffn[:, fi, :, :, 0:2048] = wg.transpose(0, 3, 2, 1, 4).reshape(L, NG, 128, 2048)
        wffn[:, fi, :, :, 2048:4096] = wu.transpose(0, 3, 2, 1, 4).reshape(L, NG, 128, 2048)
        wffn[:, fi, :, :, 4096:6144] = wd.transpose(0, 1, 3, 2, 4).reshape(L, NG, 128, 2048)
    shared["wffn"] = wffn.reshape(L * 2 * NG * 128, 6144)
    win = f(inp["w_in"]).reshape(L, 8, 128, 5, 512)
    wout = f(inp["w_out"]).reshape(L, 8, 128, 2, 512)
    wproj = np.empty((L, 7, 128, 4096), np.float32)
    wproj[:, 0:5] = win.transpose(0, 3, 2, 1, 4).reshape(L, 5, 128, 4096)
    wproj[:, 5:7] = wout.transpose(0, 3, 2, 1, 4).reshape(L, 2, 128, 4096)
    shared["wproj"] = wproj.reshape(L * 7 * 128, 4096)
    gains = np.empty((128, L * 24 + 8), np.float32)
    for l in range(L):
        for k, nm in enumerate(("ffn1_norm", "mix_norm", "ffn2_norm")):
            gains[:, l * 24 + 8 * k:l * 24 + 8 * k + 8] = f(inp[nm])[l].reshape(8, 128).T
    gains[:, L * 24:L * 24 + 8] = f(inp["final_norm"]).reshape(8, 128).T
    shared["gains"] = gains
    shared["sgw"] = np.ascontiguousarray(f(inp["sgu_w"]).transpose(0, 3, 1, 2)).reshape(L * 128, 1024)
    shared["sgn"] = np.ascontiguousarray(np.broadcast_to(f(inp["sgu_norm"]).reshape(L, 1, 512), (L, 128, 512))).reshape(L * 128, 512)
    sb = f(inp["sgu_b"]).reshape(L, 4, 2, 1, 128)
    sgb = np.broadcast_to(sb, (L, 4, 2, 64, 128)).transpose(0, 2, 3, 1, 4)
    shared["sgb"] = np.ascontiguousarray(sgb).reshape(L * 128, 512)
    lq = np.concatenate([f(inp["lambda_q1"])[:, None, :], f(inp["lambda_q2"])[:, None, :]], axis=1).reshape(1, L * 128)
    lk = np.concatenate([f(inp["lambda_k1"])[:, None, :], f(inp["lambda_k2"])[:, None, :]], axis=1).reshape(1, L * 128)
    shared["lamq"] = np.ascontiguousarray(np.broadcast_to(lq, (128, L * 128)))
    shared["lamk"] = np.ascontiguousarray(np.broadcast_to(lk, (128, L * 128)))
    shared["subln"] = np.ascontiguousarray(f(inp["diff_subln"]).T)
    x = f(inp["x"])
    in_maps = []
    for b in range(8):
        m = dict(shared)
        m["xT"] = np.ascontiguousarray(x[b].T)
        in_maps.append(m)
    return in_maps


_CACHE = {}


def kernel(**inputs):
    in_maps = prep_inputs(inputs)
    if "nc" not in _CACHE:
        _CACHE["nc"] = build()[0]
    nc = _CACHE["nc"]
    res = run_bass_kernel_spmd(nc, in_maps, core_ids=list(range(8)))
    out = np.stack([np.ascontiguousarray(r["outT"].T) for r in res.results], axis=0)
    return out.astype(np.float32)
```

```python
import math
from contextlib import ExitStack
import numpy as np
import concourse.bass as bass
import concourse.mybir as mybir
from concourse.bass_utils import run_bass_kernel_spmd

F32 = mybir.dt.float32
BF16 = mybir.dt.bfloat16
I32 = mybir.dt.int32
AF = mybir.ActivationFunctionType
ALU = mybir.AluOpType
AX = mybir.AxisListType

D = 1024
S = 2048
L = 4
DFF = 2816
NG = 11
TT = 4
TW = 512
EPS = 1e-6
NSLOT = 4
SLOT_F32 = 3072
SLOPES = [2.0 ** (-8.0 * (i + 1) / 4) for i in range(4)]


class Res:
    __slots__ = ("name", "last_w", "readers", "excl")

    def __init__(self, name, excl=False):
        self.name = name
        self.last_w = None
        self.readers = []
        self.excl = excl

    def inherit(self, olds):
        for o in olds:
            if o.last_w is not None:
                self.readers.append(o.last_w)
            self.readers.extend(o.readers)
        return self


class Op:
    __slots__ = ("eng", "fn", "deps", "stream", "idx", "signal", "sigval", "dma_sem", "name")


class Prog:
    ENGS = ("pe", "act", "dve", "pool", "sp")

    def __init__(self, nc):
        self.nc = nc
        self.ops = {e: [] for e in self.ENGS}
        self.epoch = 0
        self.stream_idx = {}
        self.known = {e: {} for e in self.ENGS}
        self.n_waits = 0

    def set_epoch(self, e):
        self.epoch = e

    def op(self, eng, fn, reads=(), writes=(), dma_sem=None, name=""):
        o = Op()
        o.eng = eng
        o.fn = fn
        o.name = name
        o.signal = False
        o.sigval = None
        o.dma_sem = dma_sem
        if dma_sem is not None:
            o.stream = ("dma", dma_sem)
            o.signal = True
        else:
            o.stream = (eng, self.epoch)
        o.idx = self.stream_idx.get(o.stream, 0) + 1
        self.stream_idx[o.stream] = o.idx
        deps = []
        raw = set()
        for r in reads:
            if r.excl:
                continue
            if r.last_w is not None:
                deps.append(r.last_w)
                raw.add(id(r.last_w))
        wr = list(writes) + [r for r in reads if r.excl]
        for w in wr:
            if w.last_w is not None:
                deps.append(w.last_w)
                if w.excl:
                    raw.add(id(w.last_w))
            deps.extend(w.readers)
        best = {}
        for d in deps:
            if d.dma_sem is None and d.eng == eng and o.dma_sem is None:
                if eng == "pe" or id(d) not in raw:
                    continue
            if d.idx <= self.known[eng].get(d.stream, 0):
                continue
            b = best.get(d.stream)
            if b is None or d.idx > b.idx:
                best[d.stream] = d
        o.deps = list(best.values())
        for d in o.deps:
            d.signal = True
            self.known[eng][d.stream] = d.idx
        for r in reads:
            if not r.excl:
                r.readers.append(o)
        for w in wr:
            w.last_w = o
            w.readers = []
        self.ops[eng].append(o)
        return o

    def emit(self, final_streams=()):
        nc = self.nc
        totals = {}
        for e in self.ENGS:
            for o in self.ops[e]:
                if o.signal:
                    c = totals.get(o.stream, 0) + 1
                    totals[o.stream] = c
                    o.sigval = c
        names = sorted(totals.keys(), key=str)
        handles = {"pe": "tensor", "act": "scalar", "dve": "vector", "pool": "gpsimd", "sp": "sync"}
        with ExitStack() as st:
            sems = {}
            for s in names:
                sems[s] = st.enter_context(nc.semaphore("s_" + "_".join(str(x) for x in s).replace(":", "_")))
            block = st.enter_context(nc.Block())

            def wait_val(d):
                if d.dma_sem is not None:
                    if d.dma_sem.startswith("all:"):
                        return totals[d.stream] * 16
                    return d.sigval * 16
                return d.sigval

            def make(e):
                ops = self.ops[e]

                def body(eng):
                    for o in ops:
                        for d in o.deps:
                            eng.wait_ge(sems[d.stream], wait_val(d))
                            self.n_waits += 1
                        ins = o.fn(eng)
                        if o.signal:
                            ins.then_inc(sems[o.stream], 16 if o.dma_sem is not None else 1)
                    if e == "sp":
                        for stream in final_streams:
                            eng.wait_ge(sems[stream], totals[stream] * 16)
                return body

            for e in self.ENGS:
                if self.ops[e] or e == "sp":
                    getattr(block, handles[e])(make(e))
        return len(names)


class Region:
    def __init__(self, handle, nf32, name):
        self.h = handle
        self.n = nf32
        self.name = name
        self.off = 0
        self.cur = []
        self.prev = []

    def reset(self):
        self.prev = self.cur + self.prev[:0]
        self.cur = []
        self.off = 0

    def carve(self, name, free_shape, dtype, nres=1, at=None, res=None):
        n = 1
        for s in free_shape:
            n *= s
        nf = (n + 1) // 2 if dtype == BF16 else n
        nf = (nf + 7) // 8 * 8
        off = self.off if at is None else at
        assert off + nf <= self.n, (self.name, name, off, nf, self.n)
        a = self.h[:, off:off + nf]
        if dtype != F32:
            a = a.bitcast(dtype)
        a = a[:, 0:n]
        if at is None:
            self.off += nf
        self.last_off = off
        if len(free_shape) == 2:
            a = a.rearrange("p (a b) -> p a b", a=free_shape[0])
        elif len(free_shape) == 3:
            a = a.rearrange("p (a b c) -> p a b c", a=free_shape[0], b=free_shape[1])
        if res is None:
            res = [Res(f"{name}{i}").inherit(self.prev) for i in range(nres)]
            self.cur.extend(res)
        return a, res


def build(n_layers=L, stop="full", dbg=""):
    nc = bass.Bass("TRN2", target_bir_lowering=False)
    xT = nc.dram_tensor("xT", [D, S], F32, kind="ExternalInput").ap()
    wffn = nc.dram_tensor("wffn", [L * 2 * NG * 128, 6144], F32, kind="ExternalInput").ap()
    wproj = nc.dram_tensor("wproj", [L * 7 * 128, 4096], F32, kind="ExternalInput").ap()
    gains_d = nc.dram_tensor("gains", [128, L * 24 + 8], F32, kind="ExternalInput").ap()
    sgw_d = nc.dram_tensor("sgw", [L * 128, 1024], F32, kind="ExternalInput").ap()
    sgn_d = nc.dram_tensor("sgn", [L * 128, 512], F32, kind="ExternalInput").ap()
    sgb_d = nc.dram_tensor("sgb", [L * 128, 512], F32, kind="ExternalInput").ap()
    lamq_d = nc.dram_tensor("lamq", [128, L * 128], F32, kind="ExternalInput").ap()
    lamk_d = nc.dram_tensor("lamk", [128, L * 128], F32, kind="ExternalInput").ap()
    subln_d = nc.dram_tensor("subln", [128, L], F32, kind="ExternalInput").ap()
    outT = nc.dram_tensor("outT", [D, S], F32, kind="ExternalOutput").ap()

    P = Prog(nc)
    st = ExitStack()
    Xh = st.enter_context(nc.sbuf_tensor("X", [128, 8 * S], F32))
    RHh = st.enter_context(nc.sbuf_tensor("RH", [128, 8192], F32))
    RINGh = st.enter_context(nc.sbuf_tensor("RING", [128, NSLOT * SLOT_F32], F32))
    RMh = st.enter_context(nc.sbuf_tensor("RM", [128, 13312], F32))
    PARh = st.enter_context(nc.sbuf_tensor("PAR", [128, 1536], F32))
    CONh = st.enter_context(nc.sbuf_tensor("CON", [128, 1024], F32))
    PSh = st.enter_context(nc.psum_tensor("PS", [128, 4096], F32))

    X = Xh[:, :].rearrange("p (c t) -> p c t", c=8)
    Xres = [Res(f"X{t}") for t in range(TT)]
    bank = [PSh[:, b * 512:(b + 1) * 512] for b in range(8)]
    bres = [Res(f"bank{b}", excl=True) for b in range(8)]

    RH = Region(RHh, 8192, "RH")
    RM = Region(RMh, 13312, "RM")
    PAR = Region(PARh, 1536, "PAR")
    CON = Region(CONh, 1024, "CON")

    ONES, ONES_r = CON.carve("ones", [128], BF16)
    IDENT, IDENT_r = CON.carve("ident", [128], BF16)
    MASKB, MASKB_r = CON.carve("maskb", [128], BF16)
    TRI, TRI_r = CON.carve("tri", [128], F32)
    GAINS, GAINS_r = CON.carve("gains", [L * 24 + 8], F32)
    ALB, ALB_r = CON.carve("alb", [4, 16], F32)
    IOTA_I, IOTA_I_r = CON.carve("iota_i", [16], I32)
    IOTA_F, IOTA_F_r = CON.carve("iota_f", [16], F32)
    EPSC, EPSC_r = CON.carve("epsc", [1], F32)
    NEGLAM, NEGLAM_r = CON.carve("neglam", [L], F32)
    GSUB, GSUB_r = CON.carve("gsub", [L], F32)
    LSUM, LSUM_r = CON.carve("lsum", [2 * L], F32)
    LEXP, LEXP_r = CON.carve("lexp", [2 * L], F32)
    SUBLN, SUBLN_r = CON.carve("subln", [L], F32)
    OSQc, OSQc_r = CON.carve("osq", [TW], BF16)
    SSc, SSc_r = CON.carve("ss", [8], F32)
    SLc, SLc_r = CON.carve("sl", [8], F32)
    SRc, SRc_r = CON.carve("sr", [8], F32)
    SS2c, SS2c_r = CON.carve("ss2", [8], F32)
    SL2c, SL2c_r = CON.carve("sl2", [8], F32)
    SR2c, SR2c_r = CON.carve("sr2", [8], F32)
    SS2c_r, SL2c_r, SR2c_r = SS2c_r[0], SL2c_r[0], SR2c_r[0]
    SS4, SS4_r = CON.carve("ss4", [4, 8], F32)
    SL4, SL4_r = CON.carve("sl4", [4, 8], F32)
    SR4, SR4_r = CON.carve("sr4", [4, 8], F32)
    OSQc_r, SSc_r, SLc_r, SRc_r = OSQc_r[0], SSc_r[0], SLc_r[0], SRc_r[0]
    ONES_r, IDENT_r, MASKB_r, TRI_r, GAINS_r, ALB_r = ONES_r[0], IDENT_r[0], MASKB_r[0], TRI_r[0], GAINS_r[0], ALB_r[0]
    IOTA_I_r, IOTA_F_r, EPSC_r, NEGLAM_r, GSUB_r = IOTA_I_r[0], IOTA_F_r[0], EPSC_r[0], NEGLAM_r[0], GSUB_r[0]
    LSUM_r, LEXP_r, SUBLN_r = LSUM_r[0], LEXP_r[0], SUBLN_r[0]

    P.op("sp", lambda e: e.dma_start(out=GAINS, in_=gains_d), writes=[GAINS_r], dma_sem="all:par")
    P.op("sp", lambda e: e.dma_start(out=SUBLN, in_=subln_d), writes=[SUBLN_r], dma_sem="all:par")
    LQ, LQ_r = RM.carve("lq", [2 * L, 64], F32)
    LK, LK_r = RM.carve("lk", [2 * L, 64], F32)
    P.op("sp", lambda e: e.dma_start(out=LQ, in_=lamq_d.rearrange("p (a b) -> p a b", b=64)), writes=LQ_r, dma_sem="all:par")
    P.op("sp", lambda e: e.dma_start(out=LK, in_=lamk_d.rearrange("p (a b) -> p a b", b=64)), writes=LK_r, dma_sem="all:par")

    xsrc = xT.rearrange("(c p) t -> p c t", p=128)
    for t in range(TT):
        P.op("sp", lambda e, t=t: e.dma_start(out=X[:, :, t * TW:(t + 1) * TW], in_=xsrc[:, :, t * TW:(t + 1) * TW]),
             writes=[Xres[t]], dma_sem=f"x{t}")
    P.op("pool", lambda e: e.memset(ONES, 1.0), writes=[ONES_r])
    P.op("pool", lambda e: e.memset(IDENT, 1.0), writes=[IDENT_r])
    P.op("pool", lambda e: e.affine_select(out=IDENT, in_=IDENT, pattern=[[1, 128]], compare_op=ALU.is_equal,
                                           fill=0.0, base=0, channel_multiplier=-1),
         reads=[IDENT_r], writes=[IDENT_r])
    P.op("pool", lambda e: e.memset(MASKB, -30000.0), writes=[MASKB_r])
    P.op("pool", lambda e: e.affine_select(out=MASKB, in_=MASKB, pattern=[[-1, 128]], compare_op=ALU.is_gt,
                                           fill=0.0, base=0, channel_multiplier=1),
         reads=[MASKB_r], writes=[MASKB_r])
    P.op("pool", lambda e: e.memset(TRI, 1.0), writes=[TRI_r])
    P.op("pool", lambda e: e.affine_select(out=TRI, in_=TRI, pattern=[[1, 128]], compare_op=ALU.is_ge,
                                           fill=0.0, base=0, channel_multiplier=-1),
         reads=[TRI_r], writes=[TRI_r])
    P.op("pool", lambda e: e.memset(EPSC, EPS), writes=[EPSC_r])
    P.op("pool", lambda e: e.iota(IOTA_I, pattern=[[128, 16]], base=-14 * 128, channel_multiplier=1), writes=[IOTA_I_r])
    P.op("dve", lambda e: e.tensor_copy(out=IOTA_F, in_=IOTA_I), reads=[IOTA_I_r], writes=[IOTA_F_r])
    for h in range(4):
        P.op("dve", lambda e, h=h: e.tensor_scalar(out=ALB[:, h, :], in0=IOTA_F, scalar1=float(SLOPES[h]), scalar2=None,
                                                   op0=ALU.mult),
             reads=[IOTA_F_r], writes=[ALB_r])
    P.op("dve", lambda e: e.tensor_tensor(out=LQ, in0=LQ, in1=LK, op=ALU.mult), reads=LQ_r + LK_r, writes=LQ_r)
    P.op("dve", lambda e: e.tensor_reduce(out=LSUM, in_=LQ, axis=AX.X, op=ALU.add), reads=LQ_r, writes=[LSUM_r])
    P.op("act", lambda e: e.activation(out=LEXP, in_=LSUM, func=AF.Exp), reads=[LSUM_r], writes=[LEXP_r])
    for l in range(L):
        lam_init = 0.8 - 0.6 * math.exp(-0.3 * l)
        P.op("dve", lambda e, l=l, li=lam_init: e.tensor_scalar(out=NEGLAM[:, l:l + 1], in0=LEXP[:, 2 * l + 1:2 * l + 2],
                                                                scalar1=float(-li), scalar2=None, op0=ALU.add),
             reads=[LEXP_r], writes=[NEGLAM_r])
        P.op("dve", lambda e, l=l: e.tensor_tensor(out=NEGLAM[:, l:l + 1], in0=NEGLAM[:, l:l + 1], in1=LEXP[:, 2 * l:2 * l + 1],
                                                   op=ALU.subtract),
             reads=[LEXP_r, NEGLAM_r], writes=[NEGLAM_r])
        P.op("dve", lambda e, l=l, li=lam_init: e.tensor_scalar(out=GSUB[:, l:l + 1], in0=SUBLN[:, l:l + 1],
                                                                scalar1=float(1.0 - li), scalar2=None, op0=ALU.mult),
             reads=[SUBLN_r], writes=[GSUB_r])

    ring_res = [Res(f"slot{i}") for i in range(NSLOT)]
    loads = []
    for l in range(n_layers):
        for g in range(NG):
            loads.append(("ffn", ((l * 2 + 0) * NG + g) * 128, 6144))
        if stop == "ffn1" and l == n_layers - 1:
            break
        for pc in (2, 3):
            loads.append(("proj", (l * 7 + pc) * 128, 4096))
        for t in range(TT):
            order = [0, 1, 4] + ([2, 3] if t + 1 < TT else []) + [5, 6]
            for pc in order:
                loads.append(("proj", (l * 7 + pc) * 128, 4096))
        if stop == "mix" and l == n_layers - 1:
            break
        for g in range(NG):
            loads.append(("ffn", ((l * 2 + 1) * NG + g) * 128, 6144))
    ring_state = {"next_rec": 0, "next_use": 0, "released": 0}

    def slot_view(i, nelem):
        a = RINGh[:, i * SLOT_F32:(i + 1) * SLOT_F32].bitcast(BF16)
        return a[:, 0:nelem]

    def record_load(n):
        kind, row0, nelem = loads[n]
        si = n % NSLOT
        src = (wffn if kind == "ffn" else wproj)[row0:row0 + 128, :].rearrange("p (a b) -> p a b", b=2048)
        dst = slot_view(si, nelem).rearrange("p (a b) -> p a b", b=2048)
        P.op("pool", lambda e, dst=dst, src=src: e.dma_start(out=dst, in_=src, max_dma_last_dim=8192),
             writes=[ring_res[si]], dma_sem=f"w{si}")

    def ring_prefetch():
        while ring_state["next_rec"] < min(len(loads), ring_state["released"] + NSLOT):
            record_load(ring_state["next_rec"])
            ring_state["next_rec"] += 1

    def next_weights(kind):
        n = ring_state["next_use"]
        assert loads[n][0] == kind, (n, loads[n], kind)
        ring_prefetch()
        assert n < ring_state["next_rec"], (n, ring_state)
        ring_state["next_use"] = n + 1
        si = n % NSLOT
        return slot_view(si, loads[n][2]), ring_res[si]

    def release_weights():
        ring_state["released"] += 1
        assert ring_state["released"] <= ring_state["next_use"]
        ring_prefetch()

    def rms_stats(xtile_ap, xres, xsq, xsq_r, lbuf, lbuf_r, sb, nchunks, inv_n):
        P.op("act", lambda e: e.activation(out=xsq, in_=xtile_ap, func=AF.Square), reads=xres, writes=[xsq_r])
        for c in range(nchunks):
            rhs = xsq[:, c, :] if nchunks > 1 else xsq
            P.op("pe", lambda e, rhs=rhs, c=c: e.matmul(bank[sb], lhsT=ONES, rhs=rhs, start=(c == 0), stop=(c == nchunks - 1)),
                 reads=[ONES_r, xsq_r, bres[sb]])
        P.op("act", lambda e: e.activation(out=lbuf, in_=bank[sb], func=AF.Ln, bias=EPSC[:, 0:1], scale=float(inv_n)),
             reads=[bres[sb], EPSC_r], writes=[lbuf_r])
        P.op("act", lambda e: e.activation(out=bank[sb], in_=lbuf, func=AF.Exp, scale=-0.5),
             reads=[lbuf_r, bres[sb]])

    def norm_tile(t, gcol, xsq, xsq_r, lbuf, lbuf_r, sb, hout, hout_r):
        rms_stats(X[:, :, t * TW:(t + 1) * TW], [Xres[t]], xsq, xsq_r, lbuf, lbuf_r, sb, 8, 1.0 / D)
        for c in range(8):
            P.op("dve", lambda e, c=c: e.scalar_tensor_tensor(out=hout[:, c, :], in0=X[:, c, t * TW:(t + 1) * TW],
                                                             scalar=GAINS[:, gcol + c:gcol + c + 1], in1=bank[sb],
                                                             op0=ALU.mult, op1=ALU.mult),
                 reads=[Xres[t], GAINS_r, bres[sb]], writes=[hout_r])

    def ffn(l, f):
        RH.reset()
        RM.reset()
        H, H_r = RH.carve("H", [8, S], BF16, nres=TT)
        A = []
        for i in range(2):
            A.append(RM.carve(f"A{i}", [4, TW], BF16))
        SG = []
        for i in range(2):
            SG.append(RM.carve(f"SG{i}", [TW], BF16))
        XSQs = [RM.carve(f"XSQ{i}", [8, TW], BF16) for i in range(2)]
        LB, LB_r = RM.carve("LB", [TW], F32)
        gcol = l * 24 + (0 if f == 0 else 16)
        GB = [0, 1]
        UB = [2, 3]
        YB = [4, 5, 6, 7]
        SBK = YB[3]

        def n_square(t):
            xsq, xsq_r = XSQs[t % 2]
            P.op("act", lambda e: e.activation(out=xsq, in_=X[:, :, t * TW:(t + 1) * TW], func=AF.Square), reads=[Xres[t]], writes=xsq_r)

        def n_stat(t):
            xsq, xsq_r = XSQs[t % 2]
            for c in range(8):
                P.op("pe", lambda e, c=c: e.matmul(bank[SBK], lhsT=ONES, rhs=xsq[:, c, :], start=(c == 0), stop=(c == 7)),
                     reads=[ONES_r] + xsq_r + [bres[SBK]])
            P.op("act", lambda e: e.activation(out=LB, in_=bank[SBK], func=AF.Ln, bias=EPSC[:, 0:1], scale=1.0 / D),
                 reads=[bres[SBK], EPSC_r], writes=LB_r)
            P.op("act", lambda e: e.activation(out=bank[SBK], in_=LB, func=AF.Exp, scale=-0.5), reads=LB_r + [bres[SBK]])

        def n_apply(t):
            for c in range(8):
                P.op("dve", lambda e, c=c: e.scalar_tensor_tensor(out=H[:, c, t * TW:(t + 1) * TW], in0=X[:, c, t * TW:(t + 1) * TW],
                                                                 scalar=GAINS[:, gcol + c:gcol + c + 1], in1=bank[SBK],
                                                                 op0=ALU.mult, op1=ALU.mult),
                     reads=[Xres[t], GAINS_r, bres[SBK]], writes=[H_r[t]])

        def gu(sidx, t, Ws):
            a, a_r = A[sidx % 2]
            for gi, (W, W_r) in enumerate(Ws):
                WG = W[:, 0:2048].rearrange("p (c j) -> p c j", c=8)
                WU = W[:, 2048:4096].rearrange("p (c j) -> p c j", c=8)
                for j in range(2):
                    ci = 2 * gi + j
                    for (Wm, bk) in ((WG, GB[j]), (WU, UB[j])):
                        for c in range(8):
                            P.op("pe", lambda e, Wm=Wm, bk=bk, c=c, j=j: e.matmul(bank[bk], lhsT=Wm[:, c, j * 128:(j + 1) * 128],
                                                                                rhs=H[:, c, t * TW:(t + 1) * TW],
                                                                                start=(c == 0), stop=(c == 7)),
                                 reads=[W_r, H_r[t], bres[bk]])
                    sg, sg_r = SG[j]
                    P.op("act", lambda e, sg=sg, j=j: e.activation(out=sg, in_=bank[GB[j]], func=AF.Silu),
                         reads=[bres[GB[j]]], writes=sg_r)
                    P.op("dve", lambda e, sg=sg, j=j, a=a, ci=ci: e.tensor_tensor(out=a[:, ci, :], in0=bank[UB[j]], in1=sg, op=ALU.mult),
                         reads=[bres[UB[j]]] + sg_r, writes=a_r)

        ycount = [0]

        def yphase(sidx, t, Ws):
            a, a_r = A[sidx % 2]
            nch = 2 * len(Ws)
            for d in range(8):
                bk = YB[ycount[0] % 3]
                ycount[0] += 1
                k = 0
                for gi, (W, W_r) in enumerate(Ws):
                    WD = W[:, 4096:6144].rearrange("p (j d) -> p j d", j=2)
                    for j in range(2):
                        ci = 2 * gi + j
                        P.op("pe", lambda e, bk=bk, j=j, d=d, WD=WD, ci=ci, k=k: e.matmul(
                            bank[bk], lhsT=WD[:, j, d * 128:(d + 1) * 128], rhs=a[:, ci, :],
                            start=(k == 0), stop=(k == nch - 1)),
                            reads=[W_r] + a_r + [bres[bk]])
                        k += 1
                P.op("dve", lambda e, bk=bk, d=d: e.scalar_tensor_tensor(out=X[:, d, t * TW:(t + 1) * TW], in0=bank[bk], scalar=0.5,
                                                                         in1=X[:, d, t * TW:(t + 1) * TW],
                                                                         op0=ALU.mult, op1=ALU.add),
                     reads=[bres[bk], Xres[t]], writes=[Xres[t]])

        n_square(0)
        n_stat(0)
        n_apply(0)
        n_square(1)
        prev = None
        sidx = 0
        groups = list(range(NG))
        pairs = [groups[i:i + 2] for i in range(0, NG, 2)]
        for pi, pr in enumerate(pairs):
            Ws = [next_weights("ffn") for _ in pr]
            for t in range(TT):
                if pi == 0 and t + 1 < TT:
                    n_stat(t + 1)
                    n_apply(t + 1)
                    if t + 2 < TT:
                        n_square(t + 2)
                gu(sidx, t, Ws)
                if prev is not None:
                    yphase(*prev[:3])
                    if prev[1] == TT - 1:
                        for _ in prev[2]:
                            release_weights()
                prev = (sidx, t, Ws)
                sidx += 1
        yphase(*prev[:3])
        for _ in prev[2]:
            release_weights()

    def mixer(l):
        RH.reset()
        RM.reset()
        HT, HT_r = RH.carve("HT", [8, TW], BF16)
        K, K_r = RH.carve("K", [4, S], BF16, nres=TT)
        QZ, QZ_r = RH.carve("QZ", [2, 4, TW], BF16)
        V, V_r = RM.carve("V", [16, 512], BF16, nres=16)
        YM, YM_r = RM.carve("YM", [8, TW], BF16)
        ym_off = RM.last_off
        E = [[RM.carve(f"E{m}{i}", [TW], BF16) for i in range(2)] for m in range(2)]
        VN = [[RM.carve(f"VN{m}{i}", [512], BF16) for i in range(2)] for m in range(2)]
        XSQ, _ = RM.carve("XSQm", [8, TW], BF16, at=ym_off, res=YM_r)
        UG, U8_r = RM.carve("UG", [4, TW], F32)
        u8 = RM.last_off
        OOH, _ = RM.carve("OOH", [4, TW], F32, at=u8, res=U8_r)
        WST, _ = RM.carve("WST", [8, 128], F32, at=u8, res=U8_r)
        LB, LB_r = RM.carve("LBm", [TW], F32)
        VGs = [RM.carve(f"VG{i}", [8, 64], F32) for i in range(2)]
        RR, _ = RM.carve("RR", [2 * TW], F32, at=RM.last_off - 512, res=VGs[0][1])
        RR_r = VGs[0][1] + VGs[1][1]
        TMPs = [RM.carve(f"TMP{i}", [8, 64], F32) for i in range(2)]
        T2, _ = RM.carve("T2", [TW], F32, at=RM.last_off, res=TMPs[1][1])
        T2_r = TMPs[1][1]
        GG, GG_r = RM.carve("GG", [TW], F32)
        OSQ, OSQ_r = OSQc, [OSQc_r]
        PAR.reset()
        WS, WS_r = PAR.carve("WS", [8, 128], BF16)
        SGN, SGN_r = PAR.carve("SGN", [8, 64], F32)
        SGB, SGB_r = PAR.carve("SGB", [4, 128], F32)
        gcol = l * 24 + 8

        P.op("sp", lambda e: e.dma_start(out=SGN, in_=sgn_d[l * 128:(l + 1) * 128, :].rearrange("p (a b) -> p a b", a=8)),
             writes=SGN_r, dma_sem=f"all:lp{l}")
        P.op("sp", lambda e: e.dma_start(out=SGB, in_=sgb_d[l * 128:(l + 1) * 128, :].rearrange("p (a b) -> p a b", a=4)),
             writes=SGB_r, dma_sem=f"all:lp{l}")
        P.op("sp", lambda e: e.dma_start(out=WST, in_=sgw_d[l * 128:(l + 1) * 128, :].rearrange("p (a b) -> p a b", a=8)),
             writes=U8_r, dma_sem=f"all:lp{l}")
        P.op("dve", lambda e: e.tensor_tensor(out=WS, in0=WST, in1=TRI.unsqueeze(1).broadcast_to([128, 8, 128]), op=ALU.mult),
             reads=U8_r + [TRI_r], writes=WS_r)
        P.op("pool", lambda e: e.memset(QZ, 0.0), writes=QZ_r)
        for m in range(2):
            for i in range(2):
                P.op("pool", lambda e, m=m, i=i: e.memset(VN[m][i][0], 0.0), writes=VN[m][i][1])

        cp = [0]
        pending = [False]

        def rel():
            if pending[0]:
                release_weights()
                pending[0] = False

        def nxt():
            rel()
            pending[0] = True
            return next_weights("proj")

        def evac(out, src_bank, src_ap, out_res, reads_extra=()):
            if cp[0] % 2 == 0:
                P.op("dve", lambda e: e.tensor_copy(out=out, in_=src_ap), reads=[bres[src_bank]] + list(reads_extra), writes=out_res)
            else:
                P.op("act", lambda e: e.copy(out=out, in_=src_ap), reads=[bres[src_bank]] + list(reads_extra), writes=out_res)
            cp[0] += 1

        def norm_stats(t, sbk):
            xt = X[:, :, t * TW:(t + 1) * TW]
            P.op("act", lambda e: e.activation(out=XSQ, in_=xt, func=AF.Square), reads=[Xres[t]], writes=YM_r)
            for c in range(8):
                P.op("pe", lambda e, c=c: e.matmul(bank[sbk], lhsT=ONES, rhs=XSQ[:, c, :], start=(c == 0), stop=(c == 7)),
                     reads=[ONES_r] + YM_r + [bres[sbk]])
            P.op("act", lambda e: e.activation(out=bank[sbk], in_=bank[sbk], func=AF.Ln, bias=EPSC[:, 0:1], scale=1.0 / D),
                 reads=[bres[sbk], EPSC_r])
            P.op("act", lambda e: e.activation(out=LB, in_=bank[sbk], func=AF.Exp, scale=-0.5), reads=[bres[sbk]], writes=LB_r)

        def h_stts(t):
            for c in range(8):
                P.op("dve", lambda e, c=c: e.scalar_tensor_tensor(out=HT[:, c, :], in0=X[:, c, t * TW:(t + 1) * TW],
                                                                 scalar=GAINS[:, gcol + c:gcol + c + 1], in1=LB,
                                                                 op0=ALU.mult, op1=ALU.mult),
                     reads=[Xres[t], GAINS_r] + LB_r, writes=HT_r)

        def q_proj(t):
            W, W_r = nxt()
            WC = W.rearrange("p (c j) -> p c j", c=8)
            for h in range(4):
                bk = 4 + h
                for c in range(8):
                    P.op("pe", lambda e, bk=bk, c=c, h=h, WC=WC: e.matmul(bank[bk], lhsT=WC[:, c, h * 128:(h + 1) * 128], rhs=HT[:, c, :],
                                                                      start=(c == 0), stop=(c == 7)),
                         reads=[W_r, HT_r[0], bres[bk]])
            for h in range(4):
                bk = 4 + h
                cp[0] = 0
                evac(QZ[0:64, 0, h, :], bk, bank[bk][0:64, :], QZ_r)
                cp[0] = 0
                evac(QZ[64:128, 1, h, :], bk, bank[bk][64:128, :], QZ_r)

        def k_proj(t):
            W, W_r = nxt()
            WD_ = W.rearrange("p (c j) -> p c j", c=8)
            for h in range(4):
                bk = 4 + h
                for c in range(8):
                    P.op("pe", lambda e, bk=bk, c=c, h=h, WD_=WD_: e.matmul(bank[bk], lhsT=WD_[:, c, h * 128:(h + 1) * 128], rhs=HT[:, c, :],
                                                                        start=(c == 0), stop=(c == 7)),
                         reads=[W_r, HT_r[0], bres[bk]])
            for h in range(4):
                bk = 4 + h
                evac(K[:, h, t * TW:(t + 1) * TW], bk, bank[bk], [K_r[t]])

        norm_stats(0, 0)
        h_stts(0)
        q_proj(0)
        k_proj(0)
        for t in range(TT):
            tok = slice(t * TW, (t + 1) * TW)
            W, W_r = nxt()
            WA = W.rearrange("p (c j) -> p c j", c=8)
            for j in range(4):
                bk = 4 + j % 2
                for c in range(8):
                    P.op("pe", lambda e, bk=bk, c=c, j=j, WA=WA: e.matmul(bank[bk], lhsT=WA[:, c, j * 128:(j + 1) * 128], rhs=HT[:, c, :],
                                                                      start=(c == 0), stop=(c == 7)),
                         reads=[W_r, HT_r[0], bres[bk]])
                P.op("act", lambda e, bk=bk, j=j: e.activation(out=UG[:, j, :], in_=bank[bk], func=AF.Gelu),
                     reads=[bres[bk]], writes=U8_r)
            W, W_r = nxt()
            WB = W.rearrange("p (c j) -> p c j", c=8)
            for i in range(4):
                for c in range(8):
                    P.op("pe", lambda e, c=c, i=i, WB=WB: e.matmul(bank[i], lhsT=HT[:, c, i * 128:(i + 1) * 128], rhs=WB[:, c, :],
                                                              start=(c == 0), stop=(c == 7)),
                         reads=[W_r, HT_r[0], bres[i]])
            for i in range(4):
                tm, tm_r = TMPs[i % 2]
                b3 = bank[i].rearrange("p (a b) -> p a b", a=8)
                P.op("act", lambda e, b3=b3: e.activation(out=b3, in_=b3, func=AF.Gelu), reads=[bres[i]])
                P.op("act", lambda e, b3=b3, tm=tm: e.activation(out=tm, in_=b3, func=AF.Square), reads=[bres[i]], writes=tm_r)
                P.op("dve", lambda e, tm=tm, i=i: e.tensor_reduce(out=SS4[:, i, :], in_=tm, axis=AX.X, op=ALU.add), reads=tm_r, writes=SS4_r)
            P.op("act", lambda e: e.activation(out=SL4, in_=SS4, func=AF.Ln, bias=EPSC[:, 0:1], scale=1.0 / 64), reads=SS4_r + [EPSC_r], writes=SL4_r)
            P.op("act", lambda e: e.activation(out=SR4, in_=SL4, func=AF.Exp, scale=-0.5), reads=SL4_r, writes=SR4_r)

            def vn_chain(i):
                tm, tm_r = TMPs[i % 2]
                b3 = bank[i].rearrange("p (a b) -> p a b", a=8)
                P.op("dve", lambda e: e.tensor_tensor(out=tm, in0=b3, in1=SR4[:, i, :].unsqueeze(2).broadcast_to([128, 8, 64]), op=ALU.mult),
                     reads=[bres[i]] + SR4_r, writes=tm_r)
                vna, vna_r = VN[0][i % 2]
                vnb, vnb_r = VN[1][i % 2]
                T4 = tm.rearrange("p (a two) d -> p a two d", two=2)
                G4 = SGN.rearrange("p (a two) d -> p a two d", two=2)
                A4 = vna.rearrange("p (a two d) -> p a two d", two=2, d=64)
                B4 = vnb.rearrange("p (a two d) -> p a two d", two=2, d=64)
                P.op("dve", lambda e: e.tensor_tensor(out=A4[:, :, 0, :], in0=T4[:, :, 0, :], in1=G4[:, :, 0, :], op=ALU.mult),
                     reads=tm_r + SGN_r, writes=vna_r)
                P.op("dve", lambda e: e.tensor_tensor(out=B4[:, :, 1, :], in0=T4[:, :, 1, :], in1=G4[:, :, 1, :], op=ALU.mult),
                     reads=tm_r + SGN_r, writes=vnb_r)

            def gates(i):
                vna, vna_r = VN[0][i % 2]
                vnb, vnb_r = VN[1][i % 2]
                for j in range(4):
                    gb = 4 + j
                    P.op("pe", lambda e, gb=gb, j=j: e.matmul(bank[gb][:, i * 128:(i + 1) * 128], lhsT=vna[:, j * 128:(j + 1) * 128],
                                                            rhs=WS[:, 2 * j, :], start=True, stop=False),
                         reads=vna_r + WS_r + [bres[gb]])
                    P.op("pe", lambda e, gb=gb, j=j: e.matmul(bank[gb][:, i * 128:(i + 1) * 128], lhsT=vnb[:, j * 128:(j + 1) * 128],
                                                            rhs=WS[:, 2 * j + 1, :], start=False, stop=True),
                         reads=vnb_r + WS_r + [bres[gb]])

            if t + 1 < TT:
                xt1 = X[:, :, (t + 1) * TW:(t + 2) * TW]
                P.op("act", lambda e, xt1=xt1: e.activation(out=XSQ, in_=xt1, func=AF.Square), reads=[Xres[t + 1]], writes=YM_r)
            W, W_r = nxt()
            WE = W.rearrange("p (c j) -> p c j", c=8)
            for i in range(4):
                bk = 4 + i
                for c in range(8):
                    P.op("pe", lambda e, bk=bk, c=c, i=i, WE=WE: e.matmul(bank[bk], lhsT=HT[:, c, i * 128:(i + 1) * 128], rhs=WE[:, c, :],
                                                                      start=(c == 0), stop=(c == 7)),
                         reads=[W_r, HT_r[0], bres[bk]])
            vn_chain(0)
            vn_chain(1)
            for i in range(4):
                evac(V[:, 4 * t + i, :], 4 + i, bank[4 + i], [V_r[4 * t + i]])
            gates(0)
            gates(1)
            vn_chain(2)
            vn_chain(3)
            if t + 1 < TT:
                for c in range(8):
                    P.op("pe", lambda e, c=c: e.matmul(bank[0], lhsT=ONES, rhs=XSQ[:, c, :], start=(c == 0), stop=(c == 7)),
                         reads=[ONES_r] + YM_r + [bres[0]])
                P.op("act", lambda e: e.activation(out=bank[0], in_=bank[0], func=AF.Ln, bias=EPSC[:, 0:1], scale=1.0 / D),
                     reads=[bres[0], EPSC_r])
                P.op("act", lambda e: e.activation(out=LB, in_=bank[0], func=AF.Exp, scale=-0.5), reads=[bres[0]], writes=LB_r)
            gates(2)
            gates(3)
            for j in range(4):
                gb = 4 + j
                P.op("dve", lambda e, gb=gb, j=j: e.tensor_tensor(out=GG.rearrange("p (a b) -> p a b", a=4),
                                                                 in0=bank[gb].rearrange("p (a b) -> p a b", a=4),
                                                                 in1=SGB[:, j, :].unsqueeze(1).broadcast_to([128, 4, 128]), op=ALU.add),
                     reads=[bres[gb]] + SGB_r, writes=GG_r)
                P.op("dve", lambda e, j=j: e.tensor_tensor(out=YM[:, j, :], in0=GG, in1=UG[:, j, :], op=ALU.mult),
                     reads=GG_r + U8_r, writes=YM_r)
            if t + 1 < TT:
                h_stts(t + 1)
            rel()
            nkt = 4 * t + 4

            def qk(p, h, kt):
                jd = kt - 4 * t
                c0 = 128 * jd if jd >= 0 else 0
                cols = slice(c0, TW)
                sb = [p % 2, 2 + p % 2]
                bcol = kt - 4 * t + 12
                for m in range(2):
                    P.op("pe", lambda e, m=m, sbm=sb[m]: e.matmul(
                        bank[sbm][:, cols], lhsT=K[:, h, kt * 128:(kt + 1) * 128], rhs=QZ[:, m, h, cols],
                        start=True, stop=(jd < 0)),
                        reads=[K_r[kt // 4]] + QZ_r + [bres[sb[m]]])
                    if jd >= 0:
                        P.op("pe", lambda e, sbm=sb[m]: e.matmul(bank[sbm][:, c0:c0 + 128], lhsT=IDENT, rhs=MASKB,
                                                                  start=False, stop=True),
                             reads=[IDENT_r, MASKB_r, bres[sb[m]]])
                    em, em_r = E[m][p % 2]
                    P.op("act", lambda e, em=em, sbm=sb[m]: e.activation(
                        out=em[:, cols], in_=bank[sbm][:, cols], func=AF.Exp, bias=ALB[:, h, bcol:bcol + 1], scale=0.125),
                        reads=[bres[sb[m]], ALB_r], writes=em_r)

            def pv(p, h, kt, nkt=nkt):
                jd = kt - 4 * t
                c0 = 128 * jd if jd >= 0 else 0
                cols = slice(c0, TW)
                for m in range(2):
                    em, em_r = E[m][p % 2]
                    P.op("pe", lambda e, m=m, em=em: e.matmul(
                        bank[4 + m][:, cols], lhsT=V[:, kt, h * 128:(h + 1) * 128], rhs=em[:, cols],
                        start=(kt == 0), stop=(kt == nkt - 1)),
                        reads=[V_r[kt]] + em_r + [bres[4 + m]])
                    P.op("pe", lambda e, m=m, em=em: e.matmul(
                        bank[6 + m][:, cols], lhsT=ONES, rhs=em[:, cols], start=(kt == 0), stop=(kt == nkt - 1)),
                        reads=[ONES_r] + em_r + [bres[6 + m]])
                if kt == nkt - 1:
                    finalize(h)

            def finalize(h):
                oo = OOH[:, h, :]
                big = SLOPES[h] * 256 > 30
                if big:
                    P.op("act", lambda e: e.copy(out=RR, in_=PSh[:, 6 * 512:8 * 512]), reads=[bres[6], bres[7]], writes=RR_r)
                else:
                    P.op("act", lambda e: e.activation(out=RR, in_=PSh[:, 6 * 512:8 * 512], func=AF.Ln), reads=[bres[6], bres[7]], writes=RR_r)
                P.op("dve", lambda e: e.tensor_copy(out=oo, in_=bank[4]), reads=[bres[4]], writes=U8_r)
                P.op("dve", lambda e: e.tensor_copy(out=T2, in_=bank[5]), reads=[bres[5]], writes=T2_r)
                if big:
                    P.op("dve", lambda e: e.reciprocal(out=RR, in_=RR), reads=RR_r, writes=RR_r)
                else:
                    P.op("act", lambda e: e.activation(out=RR, in_=RR, func=AF.Exp, scale=-1.0), reads=RR_r, writes=RR_r)
                P.op("dve", lambda e: e.tensor_tensor(out=oo, in0=oo, in1=RR[:, 0:TW], op=ALU.mult), reads=U8_r + RR_r, writes=U8_r)
                P.op("dve", lambda e: e.tensor_tensor(out=T2, in0=T2, in1=RR[:, TW:2 * TW], op=ALU.mult), reads=T2_r + RR_r, writes=T2_r)
                P.op("dve", lambda e: e.scalar_tensor_tensor(out=oo, in0=T2, scalar=NEGLAM[:, l:l + 1], in1=oo, op0=ALU.mult, op1=ALU.add),
                     reads=T2_r + U8_r + [NEGLAM_r], writes=U8_r)

            ebufs = [E[0][0], E[0][1], E[1][0], E[1][1]]

            def tail_squares():
                for h in range(4):
                    eb, eb_r = ebufs[h]
                    P.op("act", lambda e, h=h, eb=eb: e.activation(out=eb, in_=OOH[:, h, :], func=AF.Square), reads=U8_r, writes=eb_r)

            def tail_stats():
                for h in range(4):
                    eb, eb_r = ebufs[h]
                    P.op("pe", lambda e, h=h, eb=eb: e.matmul(bank[h], lhsT=ONES, rhs=eb, start=True, stop=True), reads=[ONES_r] + eb_r + [bres[h]])
                P.op("act", lambda e: e.activation(out=PSh[:, 0:2048], in_=PSh[:, 0:2048], func=AF.Ln, bias=EPSC[:, 0:1], scale=1.0 / 128),
                     reads=[bres[0], bres[1], bres[2], bres[3], EPSC_r])
                P.op("act", lambda e: e.activation(out=PSh[:, 0:2048], in_=PSh[:, 0:2048], func=AF.Exp, scale=-0.5),
                     reads=[bres[0], bres[1], bres[2], bres[3]])

            def tail_apply():
                for h in range(4):
                    P.op("dve", lambda e, h=h: e.scalar_tensor_tensor(out=YM[:, 4 + h, :], in0=OOH[:, h, :], scalar=GSUB[:, l:l + 1], in1=bank[h],
                                                                      op0=ALU.mult, op1=ALU.mult),
                         reads=U8_r + [bres[h], GSUB_r], writes=YM_r)

            seq = [(h, kt) for h in range(4) for kt in range(nkt)]
            for p, (h, kt) in enumerate(seq):
                qk(p, h, kt)
                if p > 0:
                    pv(p - 1, *seq[p - 1])
            pv(len(seq) - 1, *seq[-1])
            rel()
            tail_squares()
            if t + 1 < TT:
                q_proj(t + 1)
            tail_stats()
            tail_apply()
            if t + 1 < TT:
                k_proj(t + 1)
            if dbg == "nosgu":
                P.op("dve", lambda e: e.memset(YM[:, 0:4, :], 0.0), reads=YM_r, writes=YM_r)
            if dbg == "noattn":
                P.op("dve", lambda e: e.memset(YM[:, 4:8, :], 0.0), reads=YM_r, writes=YM_r)
            for half in range(2):
                W, W_r = nxt()
                WO = W.rearrange("p (c j) -> p c j", c=8)
                for dd in range(4):
                    d = half * 4 + dd
                    bk = 4 + d % 4
                    for c in range(8):
                        P.op("pe", lambda e, bk=bk, c=c, dd=dd, WO=WO: e.matmul(bank[bk], lhsT=WO[:, c, dd * 128:(dd + 1) * 128], rhs=YM[:, c, :],
                                                                            start=(c == 0), stop=(c == 7)),
                             reads=[W_r] + YM_r + [bres[bk]])
                    P.op("dve", lambda e, bk=bk, d=d, tok=tok: e.tensor_tensor(out=X[:, d, tok], in0=bank[bk], in1=X[:, d, tok], op=ALU.add),
                         reads=[bres[bk], Xres[t]], writes=[Xres[t]])
            rel()

    done = False
    for l in range(n_layers):
        P.set_epoch(l)
        last = (l == n_layers - 1)
        ffn(l, 0)
        if last and stop == "ffn1":
            break
        mixer(l)
        if last and stop == "mix":
            break
        ffn(l, 1)

    RH.reset()
    RM.reset()
    odst = outT.rearrange("(c p) t -> p c t", p=128)
    if stop == "full":
        XSQ, XSQ_r = RM.carve("XSQf", [8, TW], BF16)
        LB, LB_r = RM.carve("LBf", [TW], F32)
        for t in range(TT):
            tok = slice(t * TW, (t + 1) * TW)
            sb = 4 + t
            rms_stats(X[:, :, tok], [Xres[t]], XSQ, XSQ_r[0], LB, LB_r[0], sb, 8, 1.0 / D)
            for c in range(8):
                P.op("dve", lambda e, c=c, tok=tok, sb=sb: e.scalar_tensor_tensor(out=X[:, c, tok], in0=X[:, c, tok],
                                                                                scalar=GAINS[:, L * 24 + c:L * 24 + c + 1], in1=bank[sb],
                                                                                op0=ALU.mult, op1=ALU.mult),
                     reads=[Xres[t], GAINS_r, bres[sb]], writes=[Xres[t]])
            P.op("sp", lambda e, tok=tok: e.dma_start(out=odst[:, :, tok], in_=X[:, :, tok]), reads=[Xres[t]], dma_sem="all:out")
    else:
        for t in range(TT):
            tok = slice(t * TW, (t + 1) * TW)
            P.op("sp", lambda e, tok=tok: e.dma_start(out=odst[:, :, tok], in_=X[:, :, tok]), reads=[Xres[t]], dma_sem="all:out")
    assert ring_state["next_use"] == len(loads), (ring_state, len(loads))
    nsem = P.emit(final_streams=[("dma", "all:out")])
    st.close()
    return nc, P, nsem


def prep_inputs(inp):
    f = lambda a: np.ascontiguousarray(np.asarray(a, dtype=np.float32))
    shared = {}
    wffn = np.empty((L, 2, NG, 128, 6144), np.float32)
    for fi, pre in enumerate(("ffn1", "ffn2")):
        wg = f(inp[pre + "_w_gate"]).reshape(L, 8, 128, NG, 256)
        wu = f(inp[pre + "_w_up"]).reshape(L, 8, 128, NG, 256)
        wd = f(inp[pre + "_w_down"]).reshape(L, NG, 2, 128, 1024)
        wffn[:, fi, :, :, 0:2048] = wg.transpose(0, 3, 2, 1, 4).reshape(L, NG, 128, 2048)
        wffn[:, fi, :, :, 2048:4096] = wu.transpose(0, 3, 2, 1, 4).reshape(L, NG, 128, 2048)
        wffn[:, fi, :, :, 4096:6144] = wd.transpose(0, 1, 3, 2, 4).reshape(L, NG, 128, 2048)
    shared["wffn"] = wffn.reshape(L * 2 * NG * 128, 6144)
    win = f(inp["w_in"]).reshape(L, 8, 128, 5, 512)
    wout = f(inp["w_out"]).reshape(L, 8, 128, 2, 512)
    wproj = np.empty((L, 7, 128, 4096), np.float32)
    wproj[:, 0:5] = win.transpose(0, 3, 2, 1, 4).reshape(L, 5, 128, 4096)
    wproj[:, 5:7] = wout.transpose(0, 3, 2, 1, 4).reshape(L, 2, 128, 4096)
    shared["wproj"] = wproj.reshape(L * 7 * 128, 4096)
    gains = np.empty((128, L * 24 + 8), np.float32)
    for l in range(L):
        for k, nm in enumerate(("ffn1_norm", "mix_norm", "ffn2_norm")):
            gains[:, l * 24 + 8 * k:l * 24 + 8 * k + 8] = f(inp[nm])[l].reshape(8, 128).T
    gains[:, L * 24:L * 24 + 8] = f(inp["final_norm"]).reshape(8, 128).T
    shared["gains"] = gains
    shared["sgw"] = np.ascontiguousarray(f(inp["sgu_w"]).transpose(0, 3, 1, 2)).reshape(L * 128, 1024)
    shared["sgn"] = np.ascontiguousarray(np.broadcast_to(f(inp["sgu_norm"]).reshape(L, 1, 512), (L, 128, 512))).reshape(L * 128, 512)
    sb = f(inp["sgu_b"]).reshape(L, 4, 2, 1, 128)
    sgb = np.broadcast_to(sb, (L, 4, 2, 64, 128)).transpose(0, 2, 3, 1, 4)
    shared["sgb"] = np.ascontiguousarray(sgb).reshape(L * 128, 512)
    lq = np.concatenate([f(inp["lambda_q1"])[:, None, :], f(inp["lambda_q2"])[:, None, :]], axis=1).reshape(1, L * 128)
    lk = np.concatenate([f(inp["lambda_k1"])[:, None, :], f(inp["lambda_k2"])[:, None, :]], axis=1).reshape(1, L * 128)
    shared["lamq"] = np.ascontiguousarray(np.broadcast_to(lq, (128, L * 128)))
    shared["lamk"] = np.ascontiguousarray(np.broadcast_to(lk, (128, L * 128)))
    shared["subln"] = np.ascontiguousarray(f(inp["diff_subln"]).T)
    x = f(inp["x"])
    in_maps = []
    for b in range(8):
        m = dict(shared)
        m["xT"] = np.ascontiguousarray(x[b].T)
        in_maps.append(m)
    return in_maps


_CACHE = {}


def kernel(**inputs):
    in_maps = prep_inputs(inputs)
    if "nc" not in _CACHE:
        _CACHE["nc"] = build()[0]
    nc = _CACHE["nc"]
    res = run_bass_kernel_spmd(nc, in_maps, core_ids=list(range(8)))
    out = np.stack([np.ascontiguousarray(r["outT"].T) for r in res.results], axis=0)
    return out.astype(np.float32)
```
